# Optimizing a Trainium2 kernel written in Bass

```python
import math
import jax, jax.numpy as jnp
from jax import lax
import numpy as np

D_MODEL = 1024
BATCH = 16
SEQ = 2048
DEPTH = 4
DEC_BATCH = 8
DEC_SEQ = 32
PAST_LEN = 1024

CHUNK = 64
N_A = DEPTH // 2
N_B = DEPTH - N_A
HG_HEADS = 8
HG_DK = D_MODEL // HG_HEADS
HG_DV = D_MODEL // HG_HEADS
GLA_BLOCK = 16
DA_HEADS = 8
DA_HEAD_DIM = D_MODEL // (2 * DA_HEADS)
Q_BLOCK = 128
NUM_BUCKETS = 32
MAX_DISTANCE = 256
D_FF = 2816
CONV_W = 3
PLE_DIM = 256
EPS = 1e-6

kernel_name = 'yoco_hgrn2_diffattn_streaming_step'


def rmsnorm(x, g):
    xf = x.astype(jnp.float32)
    y = xf * lax.rsqrt(jnp.mean(xf * xf, axis=-1, keepdims=True) + EPS)
    return (y * g.astype(jnp.float32)).astype(x.dtype)


def head_rmsnorm(x, g):
    return rmsnorm(x, g.reshape(x.shape[-2], x.shape[-1]))


def t5_bucket(rel):
    half = NUM_BUCKETS // 2
    ret = jnp.where(rel > 0, half, 0)
    n = jnp.abs(rel)
    max_exact = half // 2
    nf = jnp.maximum(n, 1).astype(jnp.float32)
    large = max_exact + (jnp.log(nf / max_exact) / math.log(MAX_DISTANCE / max_exact)
                         * (half - max_exact)).astype(jnp.int32)
    large = jnp.minimum(large, half - 1)
    return ret + jnp.where(n < max_exact, n, large)


def gla_blocked(q, k, v, logf, s0):
    B, T, H, dk = q.shape
    dv = v.shape[-1]
    pad = (-T) % GLA_BLOCK
    if pad:
        pw = ((0, 0), (0, pad), (0, 0), (0, 0))
        q, k, v, logf = [jnp.pad(a, pw) for a in (q, k, v, logf)]
    n = (T + pad) // GLA_BLOCK

    def blocks(a):
        return a.astype(jnp.float32).reshape(B, n, GLA_BLOCK, H, a.shape[-1]).transpose(1, 0, 2, 3, 4)

    qs, ks, vs, ls = blocks(q), blocks(k), blocks(v), blocks(logf)
    causal = jnp.tril(jnp.ones((GLA_BLOCK, GLA_BLOCK), dtype=bool))

    def step(S, inp):
        qb, kb, vb, lb = inp
        b = jnp.cumsum(lb, axis=1)
        b_last = b[:, -1:]
        qt = qb * jnp.exp(b)
        kt = kb * jnp.exp(-b)
        a = jnp.where(causal, jnp.einsum('blhk,bmhk->bhlm', qt, kt), 0.0)
        o = jnp.einsum('blhk,bhkv->blhv', qt, S) + jnp.einsum('bhlm,bmhv->blhv', a, vb)
        kd = kb * jnp.exp(b_last - b)
        S = jnp.exp(b_last[:, 0])[..., None] * S + jnp.einsum('bmhk,bmhv->bhkv', kd, vb)
        return S, o

    S, o = lax.scan(step, s0.astype(jnp.float32), (qs, ks, vs, ls))
    o = o.transpose(1, 0, 2, 3, 4).reshape(B, n * GLA_BLOCK, H, dv)[:, :T]
    return o, S


def hgrn2(h, s0, w_in, lower, g_out_norm, w_out):
    B, T, _ = h.shape
    qh, fh, ih, gh = jnp.split(h @ w_in, 4, axis=-1)
    fp = fh.astype(jnp.float32)
    logf = jnp.logaddexp(jnp.log(lower), jnp.log1p(-lower) + jax.nn.log_sigmoid(fp))
    kk = (1.0 - lower) * jax.nn.sigmoid(-fp)
    shp = (B, T, HG_HEADS, HG_DK)
    o, s = gla_blocked(jax.nn.silu(qh).reshape(shp), kk.reshape(shp),
                       ih.reshape(B, T, HG_HEADS, HG_DV), logf.reshape(shp), s0)
    o = head_rmsnorm(o, g_out_norm).astype(h.dtype) * jax.nn.silu(gh).reshape(B, T, HG_HEADS, HG_DV)
    return o.reshape(B, T, D_MODEL) @ w_out, s.astype(s0.dtype)


def diff_attn_core(q, k, v, q_pos, k_pos, rel_bias, lam):
    B, K = k.shape[0], k.shape[1]
    k = k.reshape(B, K, DA_HEADS, 2, DA_HEAD_DIM)
    s = jnp.einsum('bqhcd,bkhcd->bchqk', q, k).astype(jnp.float32) * (DA_HEAD_DIM ** -0.5)
    rel = k_pos[None, :] - q_pos[:, None]
    bias = jnp.transpose(rel_bias[t5_bucket(rel)], (2, 0, 1)).astype(jnp.float32)
    mask = (k_pos[None, :] // CHUNK) <= (q_pos[:, None] // CHUNK)
    p = jax.nn.softmax(jnp.where(mask, s + bias, -jnp.inf), axis=-1)
    w = p[:, 0] - lam * p[:, 1]
    return jnp.einsum('bhqk,bkhe->bqhe', w.astype(v.dtype), v)


def diff_attention(h, k_all, v_all, q_pos, k_pos, w_q, lam_p, g_subln, w_out, rel_bias, layer_idx, blocked):
    B, T, _ = h.shape
    q = (h @ w_q).reshape(B, T, DA_HEADS, 2, DA_HEAD_DIM)
    lam_init = 0.8 - 0.6 * math.exp(-0.3 * layer_idx)
    lp = lam_p.astype(jnp.float32)
    lam = jnp.exp(jnp.sum(lp[0] * lp[1])) - jnp.exp(jnp.sum(lp[2] * lp[3])) + lam_init
    if blocked:
        nb = T // Q_BLOCK
        qb = q.reshape(B, nb, Q_BLOCK, DA_HEADS, 2, DA_HEAD_DIM).transpose(1, 0, 2, 3, 4, 5)
        qpb = q_pos.reshape(nb, Q_BLOCK)
        o = lax.map(lambda a: diff_attn_core(a[0], k_all, v_all, a[1], k_pos, rel_bias, lam), (qb, qpb))
        o = o.transpose(1, 0, 2, 3, 4).reshape(B, T, DA_HEADS, 2 * DA_HEAD_DIM)
    else:
        o = diff_attn_core(q, k_all, v_all, q_pos, k_pos, rel_bias, lam)
    o = head_rmsnorm(o, g_subln) * (1.0 - lam_init)
    return o.reshape(B, T, D_MODEL) @ w_out


def conv_ffn(h, buf, w_in, cw, cb, w_out):
    g, u = jnp.split(h @ w_in, 2, axis=-1)
    T = g.shape[1]
    gp = jnp.concatenate([buf.astype(g.dtype), g], axis=1)
    c = cb
    for j in range(CONV_W):
        c = c + cw[j] * gp[:, j:j + T]
    return (jax.nn.silu(c) * u) @ w_out, gp[:, -(CONV_W - 1):]


def trunk(x, ple, pos, past_k, past_v, hg_s0, conv_s0, blocked,
          g_norms, w_in_a, lb_raw, g_hg, w_out_a, g_kv, w_kv, rel_bias,
          w_q_b, lam_b, g_subln, w_out_b, w_ffn_in, conv_w, conv_b, w_ffn_out,
          w_ple, w_ple_gate):
    B, T, _ = x.shape
    sm = jax.nn.softmax(lb_raw.astype(jnp.float32), axis=0)
    cs = jnp.cumsum(sm, axis=0)
    lower = cs - cs[0:1]
    hg_out, conv_out = [], []
    k_all = v_all = k_new = v_new = k_pos = None
    for i in range(DEPTH):
        h = rmsnorm(x, g_norms[i, 0])
        if i < N_A:
            m, s = hgrn2(h, hg_s0[i], w_in_a[i], lower[i], g_hg[i], w_out_a[i])
            hg_out.append(s)
        else:
            j = i - N_A
            m = diff_attention(h, k_all, v_all, pos, k_pos, w_q_b[j], lam_b[j], g_subln[j],
                               w_out_b[j], rel_bias, i, blocked)
        x = x + rmsnorm(m, g_norms[i, 1])
        h = rmsnorm(x, g_norms[i, 2])
        f, cbuf = conv_ffn(h, conv_s0[i], w_ffn_in[i], conv_w[i], conv_b[i], w_ffn_out[i])
        conv_out.append(cbuf)
        x = x + rmsnorm(f, g_norms[i, 3])
        x = x + jax.nn.sigmoid(x @ w_ple_gate[i]) * (ple[i] @ w_ple[i])
        if i == N_A - 1:
            kn, vn = jnp.split(rmsnorm(x, g_kv) @ w_kv, 2, axis=-1)
            k_new = kn.reshape(B, T, DA_HEADS, 2 * DA_HEAD_DIM)
            v_new = vn.reshape(B, T, DA_HEADS, 2 * DA_HEAD_DIM)
            if past_k is None:
                k_all, v_all, k_pos = k_new, v_new, pos
            else:
                k_all = jnp.concatenate([past_k.astype(k_new.dtype), k_new], axis=1)
                v_all = jnp.concatenate([past_v.astype(v_new.dtype), v_new], axis=1)
                k_pos = jnp.arange(past_k.shape[1] + T, dtype=jnp.int32)
    return x, k_new, v_new, jnp.stack(hg_out), jnp.stack(conv_out)


def setup_inputs(seed: int = 0) -> dict:
    key = jax.random.key(seed)
    ks = jax.random.split(key, 32)
    nrm = jax.random.normal
    f32 = jnp.float32
    d = {}
    d['x_prompt'] = nrm(ks[0], (BATCH, SEQ, D_MODEL), f32)
    d['x_sample'] = nrm(ks[1], (DEC_BATCH, DEC_SEQ, D_MODEL), f32)
    d['cache_k'] = nrm(ks[2], (DEC_BATCH, PAST_LEN, DA_HEADS, 2 * DA_HEAD_DIM), f32)
    d['cache_v'] = nrm(ks[3], (DEC_BATCH, PAST_LEN, DA_HEADS, 2 * DA_HEAD_DIM), f32)
    d['state_hgrn'] = 0.5 * nrm(ks[4], (N_A, DEC_BATCH, HG_HEADS, HG_DK, HG_DV), f32)
    d['state_conv'] = nrm(ks[5], (DEPTH, DEC_BATCH, CONV_W - 1, D_FF), f32)
    d['p_prompt'] = nrm(ks[6], (DEPTH, BATCH, SEQ, PLE_DIM), f32)
    d['p_sample'] = nrm(ks[7], (DEPTH, DEC_BATCH, DEC_SEQ, PLE_DIM), f32)
    sd = D_MODEL ** -0.5
    d['g_norms'] = 1.0 + 0.02 * nrm(ks[8], (DEPTH, 4, D_MODEL), f32)
    d['w_in_a'] = sd * nrm(ks[9], (N_A, D_MODEL, 4 * D_MODEL), f32)
    d['lb_raw'] = 0.5 * nrm(ks[10], (N_A, D_MODEL), f32)
    d['g_hg'] = 1.0 + 0.02 * nrm(ks[11], (N_A, D_MODEL), f32)
    d['w_out_a'] = sd * nrm(ks[12], (N_A, D_MODEL, D_MODEL), f32)
    d['g_kv'] = 1.0 + 0.02 * nrm(ks[13], (D_MODEL,), f32)
    d['w_kv'] = sd * nrm(ks[14], (D_MODEL, 2 * D_MODEL), f32)
    d['rel_bias'] = 0.2 * nrm(ks[15], (NUM_BUCKETS, DA_HEADS), f32)
    d['w_q_b'] = sd * nrm(ks[16], (N_B, D_MODEL, D_MODEL), f32)
    d['lam_b'] = 0.1 * nrm(ks[17], (N_B, 4, DA_HEAD_DIM), f32)
    d['g_subln'] = 1.0 + 0.02 * nrm(ks[18], (N_B, D_MODEL), f32)
    d['w_out_b'] = sd * nrm(ks[19], (N_B, D_MODEL, D_MODEL), f32)
    d['w_ffn_in'] = sd * nrm(ks[20], (DEPTH, D_MODEL, 2 * D_FF), f32)
    d['conv_w'] = (CONV_W ** -0.5) * nrm(ks[21], (DEPTH, CONV_W, D_FF), f32)
    d['conv_b'] = 0.01 * nrm(ks[22], (DEPTH, D_FF), f32)
    d['w_ffn_out'] = (D_FF ** -0.5) * nrm(ks[23], (DEPTH, D_FF, D_MODEL), f32)
    d['w_ple'] = (PLE_DIM ** -0.5) * nrm(ks[24], (DEPTH, PLE_DIM, D_MODEL), f32)
    d['w_ple_gate'] = sd * nrm(ks[25], (DEPTH, D_MODEL, D_MODEL), f32)
    return d


def reference(x_prompt, x_sample, cache_k, cache_v, state_hgrn, state_conv, p_prompt, p_sample,
              g_norms, w_in_a, lb_raw, g_hg, w_out_a, g_kv, w_kv, rel_bias,
              w_q_b, lam_b, g_subln, w_out_b, w_ffn_in, conv_w, conv_b, w_ffn_out,
              w_ple, w_ple_gate):
    weights = (g_norms, w_in_a, lb_raw, g_hg, w_out_a, g_kv, w_kv, rel_bias,
               w_q_b, lam_b, g_subln, w_out_b, w_ffn_in, conv_w, conv_b, w_ffn_out,
               w_ple, w_ple_gate)
    Bp, Tp, _ = x_prompt.shape
    Bs, Ts, _ = x_sample.shape
    pos_p = jnp.arange(Tp, dtype=jnp.int32)
    hg0 = jnp.zeros((N_A, Bp, HG_HEADS, HG_DK, HG_DV), x_prompt.dtype)
    cv0 = jnp.zeros((DEPTH, Bp, CONV_W - 1, D_FF), x_prompt.dtype)
    y_prompt, k_prompt, v_prompt, hgrn_prompt, conv_prompt = trunk(
        x_prompt, p_prompt, pos_p, None, None, hg0, cv0, True, *weights)
    pos_s = cache_k.shape[1] + jnp.arange(Ts, dtype=jnp.int32)
    y_sample, k_sample, v_sample, hgrn_sample, conv_sample = trunk(
        x_sample, p_sample, pos_s, cache_k, cache_v, state_hgrn, state_conv, False, *weights)
    return (y_prompt, y_sample, k_prompt, v_prompt, k_sample, v_sample,
            hgrn_prompt, hgrn_sample, conv_prompt, conv_sample)
```

```python
import math
from contextlib import ExitStack

import numpy as np
import concourse.bass as bass
import concourse.mybir as mybir
from concourse.bass_utils import run_bass_kernel_spmd

F32 = mybir.dt.float32
BF16 = mybir.dt.bfloat16
AF = mybir.ActivationFunctionType
ALU = mybir.AluOpType
AX = mybir.AxisListType

D = 1024
NCH = 8
DFF = 2816
NFF = 22
PLE = 256
H = 8
DEPTH = 4
NA = 2
EPS = 1e-6
NEG = -30000.0
ENGS = ("pe", "act", "dve", "pool", "sp")
NSLOT = 6


class Cfg:
    def __init__(self, seq=2048, tt=512, past=1024, dec=32, nprompt=2):
        self.seq, self.tt, self.past, self.dec, self.nprompt = seq, tt, past, dec, nprompt
        self.nlayers, self.stop = DEPTH, None
        self.kmax = max(seq, past + 128)


class Tl:
    __slots__ = ("name", "w", "r", "psum")

    def __init__(self, name, psum=False):
        self.name, self.w, self.r, self.psum = name, None, {}, psum


class Op:
    __slots__ = ("eng", "idx", "fn", "waits", "dwaits", "signal", "seq", "dma", "dsem", "dcum")


class Prog:
    def __init__(self, nc, es):
        self.nc, self.es = nc, es
        self.ops = {e: [] for e in ENGS}
        self.waited = {e: {f: -1 for f in ENGS} for e in ENGS}
        self.dwaited = {e: {} for e in ENGS}
        self.dsems = {}
        self.ndma = 0
        self.dry = False

    def dsem(self, name):
        if name not in self.dsems:
            self.dsems[name] = [self.es.enter_context(self.nc.semaphore("d_" + name)), 0]
        return name

    def _dep(self, o, d):
        if d is None:
            return
        if d.dma:
            if self.dwaited[o.eng].get(d.dsem, 0) >= d.dcum:
                return
            o.dwaits[d.dsem] = max(o.dwaits.get(d.dsem, 0), d.dcum)
            self.dwaited[o.eng][d.dsem] = d.dcum
            return
        if d.eng == o.eng:
            if o.eng == "pe":
                return
        if self.waited[o.eng][d.eng] >= d.idx:
            return
        o.waits[d.eng] = max(o.waits.get(d.eng, -1), d.idx)
        self.waited[o.eng][d.eng] = d.idx
        d.signal = True

    def _rec(self, eng, fn, r, w, dma=False, dsem=None, nd=1):
        if self.dry:
            return None
        o = Op()
        o.eng, o.fn, o.waits, o.dwaits, o.signal, o.seq = eng, fn, {}, {}, False, 0
        o.dma, o.dsem, o.dcum = dma, dsem, 0
        o.idx = len(self.ops[eng])
        for t in r:
            self._dep(o, t.w)
            if t.psum:
                for k, d in t.r.items():
                    if k != eng:
                        self._dep(o, d)
        for t in w:
            self._dep(o, t.w)
            for d in t.r.values():
                self._dep(o, d)
        if dma:
            self.dsems[dsem][1] += 16 * nd
            o.dcum = self.dsems[dsem][1]
            self.ndma += 1
        for t in r:
            t.r[("dma", self.ndma) if dma else eng] = o
        for t in w:
            t.w, t.r = o, {}
        self.ops[eng].append(o)
        return o

    def op(self, eng, fn, r=(), w=()):
        return self._rec(eng, fn, r, w)

    def dma(self, q, fn, r, w, sem, nd=1):
        return self._rec(q, fn, r, w, dma=True, dsem=self.dsem(sem), nd=nd)

    def emit(self):
        nc, es = self.nc, self.es
        sems = {e: es.enter_context(nc.semaphore("e_" + e)) for e in ("pe", "act", "dve", "pool")}
        for e in ENGS:
            c = 0
            for o in self.ops[e]:
                if o.signal and not o.dma:
                    c += 1
                    o.seq = c
        block = es.enter_context(nc.Block())
        ops, dsems = self.ops, self.dsems

        def run(e, eo):
            for o in ops[e]:
                for f, idx in o.waits.items():
                    eo.wait_ge(sems[f], ops[f][idx].seq)
                for key, val in o.dwaits.items():
                    eo.wait_ge(dsems[key][0], val)
                ins = o.fn(eo)
                if o.dma:
                    for i in ins:
                        i.then_inc(dsems[o.dsem][0], 16)
                elif o.signal:
                    ins.then_inc(sems[e], 1)
            if e == "sp":
                for key, (h, cum) in dsems.items():
                    if cum:
                        eo.wait_ge(h, cum)

        block.tensor(lambda t: run("pe", t))
        block.scalar(lambda a: run("act", a))
        block.vector(lambda v: run("dve", v))
        block.gpsimd(lambda g: run("pool", g))
        block.sync(lambda s: run("sp", s))


def _t5_bucket_np(rel):
    rel = np.asarray(rel, np.int64)
    half = 16
    ret = np.where(rel > 0, half, 0)
    n = np.abs(rel)
    max_exact = 8
    nf = np.maximum(n, 1).astype(np.float32)
    large = max_exact + (np.log(nf / np.float32(max_exact)) / np.float32(math.log(256 / max_exact))
                         * np.float32(half - max_exact)).astype(np.int32)
    large = np.minimum(large, half - 1)
    return ret + np.where(n < max_exact, n, large)


def _consts(cfg):
    c = {}
    c["ident"] = np.eye(128, dtype=np.float32)
    e1 = np.zeros((32, 512), np.float32)
    b = _t5_bucket_np(127 - np.arange(512))
    e1[b, np.arange(512)] = 1.0
    c["e1"] = e1
    m = np.arange(128)[:, None]
    l = np.arange(128)[None, :]
    c["tri"] = ((m // 64 == l // 64) & (l >= m)).astype(np.float32)
    sm = np.ones((128, 512), np.float32)
    sm[:, ::64] = 0.0
    c["smask"] = sm
    return c


class KB:
    def __init__(self, cfg):
        self.cfg = cfg
        self.nc = bass.Bass("TRN2", target_bir_lowering=False)
        self.es = ExitStack()
        self.P = Prog(self.nc, self.es)
        self.wq = []
        self.wpos = 0
        self.wissued = 0

    def din(self, name, shape, dt=F32):
        return self.nc.dram_tensor(name, list(shape), dt, kind="ExternalInput")

    def dout(self, name, shape, dt=F32):
        return self.nc.dram_tensor(name, list(shape), dt, kind="ExternalOutput")

    def sb(self, name, shape, dt=F32):
        t = self.es.enter_context(self.nc.sbuf_tensor(name, list(shape), dt))
        return t, Tl(name)

    def pe(self, fn, r, w):
        return self.P.op("pe", fn, r, w)

    def act(self, fn, r, w):
        return self.P.op("act", fn, r, w)

    def dve(self, fn, r, w):
        return self.P.op("dve", fn, r, w)

    def A(self, out, in_, func, r, w, bias=None, scale=None):
        kw = {}
        if bias is not None:
            kw["bias"] = bias
        if scale is not None:
            kw["scale"] = scale
        return self.act(lambda e: e.activation(out=out, in_=in_, func=func, **kw), r, w)

    def TT(self, out, a, b, op, r, w):
        return self.dve(lambda e: e.tensor_tensor(out=out, in0=a, in1=b, op=op), r, w)

    def STT(self, out, in0, scalar, in1, op0, op1, r, w):
        return self.dve(lambda e: e.scalar_tensor_tensor(out=out, in0=in0, scalar=scalar, in1=in1,
                                                         op0=op0, op1=op1), r, w)

    def TS(self, out, in0, s1, s2, op0, op1, r, w):
        if s2 is None:
            return self.dve(lambda e: e.tensor_scalar(out=out, in0=in0, scalar1=s1, scalar2=None, op0=op0), r, w)
        return self.dve(lambda e: e.tensor_scalar(out=out, in0=in0, scalar1=s1, scalar2=s2,
                                                  op0=op0, op1=op1), r, w)

    def CP(self, eng, out, in_, r, w):
        if eng == "act":
            return self.act(lambda e: e.copy(out=out, in_=in_), r, w)
        return self.P.op(eng, lambda e: e.tensor_copy(out=out, in_=in_), r, w)

    def MM(self, out, pairs, r, w, start=True, stop=True, skip=False):
        def fn(e):
            n = len(pairs)
            ins = None
            for i, (l, rh) in enumerate(pairs):
                if skip:
                    ins = e.matmul(out, lhsT=l, rhs=rh, start=(start and i == 0), stop=(stop and i == n - 1),
                                   skip_group_check=True)
                else:
                    ins = e.matmul(out, lhsT=l, rhs=rh, start=(start and i == 0), stop=(stop and i == n - 1))
            return ins
        return self.pe(fn, r, w)

    def TR(self, out, in_, k, r, w):
        ident = self.ident
        return self.pe(lambda e: e.transpose(out, in_, ident[0:k, 0:k]), r + [self.t_ident], w)

    def DMA(self, q, out, in_, r, w, sem):
        return self.P.dma(q, lambda e: [e.dma_start(out=out, in_=in_)], r, w, sem)

    def pw(self):
        i = self.pw_i
        self.pw_i = (i + 1) % 4
        return self.psw[i], self.t_psw[i]

    def pq(self):
        i = self.pq_i % self.pq_n
        self.pq_i = (i + 1) % self.pq_n
        if self.pq_n == 3:
            i = (0, 1, 3)[i]
        return self.psq[i][:, 0:128], self.t_psq[i]

    def build(self):
        cfg, nc = self.cfg, self.nc
        TT_, SEQ, PAST, DEC, NP = cfg.tt, cfg.seq, cfg.past, cfg.dec, cfg.nprompt
        KMAX = cfg.kmax
        self.xp = self.din("xp", [NP, SEQ, D])
        self.xs = self.din("xs", [1, DEC, D])
        self.ck = self.din("ck", [PAST, D])
        self.cv = self.din("cv", [PAST, D])
        self.sh = self.din("sh", [NA, H, 128, 128])
        self.sc = self.din("sc", [DEPTH, 2 * NFF, 128])
        self.pp = self.din("pp", [DEPTH, NP, SEQ, PLE])
        self.psm = self.din("psm", [DEPTH, 1, DEC, PLE])
        self.ptab = self.din("ptab", [640, 128])
        self.rb = self.din("rb", [32, H])
        self.lamb = self.din("lamb", [1, 512])
        self.w_in_a = self.din("w_in_a", [NA, D, 4 * D])
        self.w_out_a = self.din("w_out_a", [NA, D, D])
        self.w_kv = self.din("w_kv", [D, 2 * D])
        self.w_q_b = self.din("w_q_b", [2, D, D])
        self.w_out_b = self.din("w_out_b", [2, D, D])
        self.w_ffn_in = self.din("w_ffn_in", [DEPTH, D, 2 * DFF])
        self.w_ffn_out = self.din("w_ffn_out", [DEPTH, DFF, D])
        self.w_ple = self.din("w_ple", [DEPTH, PLE, D])
        self.w_ple_gate = self.din("w_ple_gate", [DEPTH, D, D])
        self.c_ident = self.din("c_ident", [128, 128])
        self.c_e1 = self.din("c_e1", [32, 512])
        self.c_tri = self.din("c_tri", [128, 128])
        self.c_smask = self.din("c_smask", [128, 512])
        self.y_p = self.dout("y_p", [NP, SEQ, D])
        self.y_s = self.dout("y_s", [1, DEC, D])
        self.k_p = self.dout("k_p", [NP, SEQ, D])
        self.v_p = self.dout("v_p", [NP, SEQ, D])
        self.k_s = self.dout("k_s", [1, DEC, D])
        self.v_s = self.dout("v_s", [1, DEC, D])
        self.hg_p = self.dout("hg_p", [NA, NP, H, 128, 128])
        self.hg_s = self.dout("hg_s", [NA, 1, H, 128, 128])
        self.cv_p = self.dout("cv_p", [DEPTH, NP, 2, DFF])
        self.cv_s = self.dout("cv_s", [DEPTH, 1, 2, DFF])
        self.scr = nc.dram_tensor("scr", [H, 128 * 513], F32, kind="Internal")
        self.t_scr = Tl("scr")

        sb = self.sb
        self.xT, self.t_xT = sb("xT", [128, NCH, TT_])
        self.t_xTc = [Tl("xT%d" % i) for i in range(NCH)]
        self.hT, self.t_hT = sb("hT", [128, NCH, TT_], BF16)
        self.aT, self.t_aT = sb("aT", [128, 11, TT_], BF16)
        self.t_aTc = [Tl("aT%d" % i) for i in range(11)]
        self.mT, self.t_mT = sb("mT", [128, NCH, TT_])
        self.t_mTc = [Tl("mT%d" % i) for i in range(NCH)]
        self.KT, _ = sb("KT", [128, H, KMAX], BF16)
        self.t_KT = [Tl("KT%d" % i) for i in range(H)]
        NKT = KMAX // 128
        self.Vr, _ = sb("Vr", [128, NKT, D], BF16)
        self.t_Vr = [Tl("Vr%d" % i) for i in range(NKT)]
        self.wsl = []
        for i in range(NSLOT):
            self.wsl.append(sb("wsl%d" % i, [128, 1408], BF16))
        self.TB, self.t_TB = sb("TB", [128, 3, H, 128])
        self.ost = [sb("ost%d" % i, [128, D]) for i in range(2)]
        self.ost_i = 0
        names = ["s", "lf", "b", "enb", "qs", "ktf", "ocp", "rst", "eb"]
        self.tf = {n: sb("t_" + n, [128, max(TT_, 512) + 2]) for n in names}
        self.vTfb = sb("t_vTf", [128, max(TT_, 512) + 2])
        self.ebl = [sb("ebl%d" % i, [128, 8]) for i in range(2)]
        self.u8, _ = sb("u8", [128, 8, TT_], BF16)
        self.t_u8 = [Tl("u8_%d" % i) for i in range(8)]
        self.sq = [sb("sq%d" % i, [128, TT_], BF16) for i in range(2)]
        self.sq_i = 0
        self.vtok = [sb("vtok0", [128, max(TT_ // 128, 1), 128], BF16)] * 2
        self.kdtok = [sb("kdtok0", [128, max(TT_ // 128, 1), 128], BF16)] * 2
        self.Am = [sb("Am0", [128, max(TT_ // 128, 1), 128], BF16)] * 2
        self.plT, self.t_plT = sb("plT", [128, 2, TT_], BF16)
        self.S, _ = sb("S", [128, NA, H, 128])
        self.t_S = [[Tl("S%d_%d" % (l, h)) for h in range(H)] for l in range(NA)]
        self.Sb, _ = sb("Sb", [128, NA, H, 128], BF16)
        self.t_Sb = [[Tl("Sb%d_%d" % (l, h)) for h in range(H)] for l in range(NA)]
        self.ptr, self.t_ptr = sb("ptr", [128, 640])
        self.ident, self.t_ident = sb("ident", [128, 128])
        self.ones, self.t_ones = sb("ones", [128, 128], BF16)
        self.onesD, self.t_onesD = sb("onesD", [128, 128], BF16)
        self.onesH, self.t_onesH = sb("onesH", [128, 128], BF16)
        self.tri, self.t_tri = sb("tri", [128, 128])
        self.smask, self.t_smask = sb("smask", [128, 512])
        self.cs, _ = sb("cs", [128, DEPTH, 2, NFF])
        self.t_cs = [Tl("cs%d" % l) for l in range(DEPTH)]
        self.oml, self.t_oml = sb("oml", [128, NA, NCH])
        self.noml, self.t_noml = sb("noml", [128, NA, NCH])
        self.farb, self.t_farb = sb("farb", [128, H])
        self.lamt, self.t_lamt = self.tf["ktf"]
        self.nlam, self.t_nlam = sb("nlam", [128, 2])
        self.gs2, self.t_gs2 = sb("gs2", [128, 2, NCH])
        self.sm8, self.t_sm8 = sb("sm8", [128, 16])
        self.rbt, self.t_rbt = sb("rbt", [32, H])
        self.rbrep, self.t_rbrep = sb("rbrep", [32, 128])
        self.psw, self.t_psw = [], []
        for i in range(4):
            self.psw.append(self.es.enter_context(nc.psum_tensor("psw%d" % i, [128, 512], F32)))
            self.t_psw.append(Tl("psw%d" % i, psum=True))
        self.psq, self.t_psq = [], [Tl("psq%d" % i, psum=True) for i in range(4)]
        for i in range(4):
            self.psq.append(self.es.enter_context(nc.psum_tensor("psq%d" % i, [128, 512], F32)))
        self.pw_i = 0
        self.pq_i = 0
        self.pq_n = 4
        self.sc_i = 0

        self.o_gn = 0
        self.o_lb = 128
        self.o_ghg = 144
        self.o_gkv = 160
        self.o_gsl = 168
        self.o_cw = 184
        self.o_cb = 448

        seqs = [("p", i) for i in range(NP)] + [("s", 0)]
        self.wsched = []
        self.P.dry = True
        for kind, si in seqs:
            self.run_seq(kind, si)
        self.P.dry = False
        self.pw_i = self.pq_i = self.ost_i = self.sq_i = 0
        self.wpos = 0
        self.setup()
        for kind, si in seqs:
            self.run_seq(kind, si)
        assert self.wpos == len(self.wsched), (self.wpos, len(self.wsched))
        self.P.emit()
        self.es.close()
        return nc

    def pcol(self, off, n=1):
        return self.ptr[:, off:off + n]

    def setup(self):
        P = self.P
        self.DMA("sp", self.ident[:], self.c_ident.ap(), [], [self.t_ident], "c0")
        self.DMA("sp", self.tri[:], self.c_tri.ap(), [], [self.t_tri], "c1")
        self.DMA("sp", self.smask[:], self.c_smask.ap(), [], [self.t_smask], "c2")
        self.DMA("sp", self.farb[:], self.rb.ap()[15:16, :].partition_broadcast(128), [], [self.t_farb], "c3")
        self.DMA("sp", self.lamt[:, 0:512], self.lamb.ap().partition_broadcast(128), [], [self.t_lamt], "c4")
        self.DMA("sp", self.rbt[:], self.rb.ap(), [], [self.t_rbt], "c5")
        self.dve(lambda e: e.memset(self.ones[:], 1.0), [], [self.t_ones])
        self.dve(lambda e: e.memset(self.onesD[:], 1.0 / D), [], [self.t_onesD])
        self.dve(lambda e: e.memset(self.onesH[:], 1.0 / 128), [], [self.t_onesH])
        for i in range(5):
            st, t_st = self.ost[i % 2]
            self.DMA("sp", st[:, 0:128], self.ptab.ap()[i * 128:(i + 1) * 128, :], [], [t_st], "ost%d" % (i % 2))
            ps, t_ps = self.pq()
            self.TR(ps, st[:, 0:128], 128, [t_st], [t_ps])
            self.CP("dve", self.ptr[:, i * 128:(i + 1) * 128], ps, [t_ps], [self.t_ptr])
        self.dve(lambda e: e.memset(self.oml[:, 0, :], 1.0), [], [self.t_oml])
        self.TT(self.sm8[:, 0:8], self.pcol(self.o_lb, 8), self.pcol(self.o_lb + 8, 8), ALU.subtract,
                [self.t_ptr], [self.t_sm8])
        self.A(self.oml[:, 1, :], self.sm8[:, 0:8], AF.Sigmoid, [self.t_sm8], [self.t_oml])
        self.TS(self.noml[:].rearrange("p a b -> p (a b)"), self.oml[:].rearrange("p a b -> p (a b)"),
                -1.0, None, ALU.mult, None, [self.t_oml], [self.t_noml])
        for j in range(2):
            lam_init = 0.8 - 0.6 * math.exp(-0.3 * (NA + j))
            base = j * 256
            self.TT(self.lamt[:, base:base + 64], self.lamt[:, base:base + 64], self.lamt[:, base + 64:base + 128],
                    ALU.mult, [self.t_lamt], [self.t_lamt])
            self.TT(self.lamt[:, base + 128:base + 192], self.lamt[:, base + 128:base + 192],
                    self.lamt[:, base + 192:base + 256], ALU.mult, [self.t_lamt], [self.t_lamt])
            self.dve(lambda e, b=base: e.reduce_sum(out=self.sm8[:, 8:9], in_=self.lamt[:, b:b + 64], axis=AX.X),
                     [self.t_lamt], [self.t_sm8])
            self.dve(lambda e, b=base: e.reduce_sum(out=self.sm8[:, 9:10], in_=self.lamt[:, b + 128:b + 192], axis=AX.X),
                     [self.t_lamt], [self.t_sm8])
            self.A(self.sm8[:, 10:12], self.sm8[:, 8:10], AF.Exp, [self.t_sm8], [self.t_sm8])
            self.TT(self.sm8[:, 12:13], self.sm8[:, 11:12], self.sm8[:, 10:11], ALU.subtract, [self.t_sm8], [self.t_sm8])
            self.TS(self.nlam[:, j:j + 1], self.sm8[:, 12:13], -lam_init, None, ALU.add, None, [self.t_sm8], [self.t_nlam])
            self.TS(self.gs2[:, j, :], self.pcol(self.o_gsl + j * 8, 8), 1.0 - lam_init, None, ALU.mult, None,
                    [self.t_ptr], [self.t_gs2])
        e1, t_e1 = self.tf["s"]
        self.DMA("sp", e1[0:32, 0:512], self.c_e1.ap(), [], [t_e1], "c6")
        for h in range(H):
            self.CP("dve", self.rbrep[:], self.rbt[:, h:h + 1].to_broadcast([32, 128]), [self.t_rbt], [self.t_rbrep])
            ps, t_ps = self.pw()
            self.MM(ps[:, 0:512], [(self.rbrep[:], e1[0:32, 0:512])], [self.t_rbrep, t_e1], [t_ps])
            rep, t_rep = self.tf["lf" if h % 2 else "b"]
            self.CP("dve", rep[:, 0:512], ps[:, 0:512], [t_ps], [t_rep])
            dst = bass.AP(self.scr, h * 128 * 513, [[513, 128], [1, 512]])
            self.DMA("sp", dst, rep[:, 0:512], [t_rep], [self.t_scr], "scrw")
        for h in range(H):
            for vi in range(3):
                src = bass.AP(self.scr, h * 128 * 513 + 127 + 128 * vi, [[512, 128], [1, 128]])
                self.DMA("sp", self.TB[:, vi, h, :], src, [self.t_scr], [self.t_TB], "tb")
        self.dve(lambda e: e.memset(self.TB[64:128, 0, :, 0:64], NEG), [], [self.t_TB])

    def issue_w(self, i):
        desc = self.wsched[i]
        sl, t_sl = self.wsl[i % NSLOT]
        kind = desc[0]
        kc = 8
        if kind == "in":
            _, l, h, g = desc
            src = self.w_in_a.ap()[l, :, g * D + h * 128: g * D + (h + 1) * 128]
        elif kind == "sq":
            _, name, l, cb = desc
            src = getattr(self, name).ap()[l, :, cb * 128:(cb + 1) * 128]
        elif kind == "kv":
            _, cb = desc
            src = self.w_kv.ap()[:, cb * 128:(cb + 1) * 128]
        elif kind == "fi":
            _, l, j, g = desc
            src = self.w_ffn_in.ap()[l, :, g * DFF + j * 128: g * DFF + (j + 1) * 128]
        elif kind == "fo":
            _, l, kg, cb = desc
            src = self.w_ffn_out.ap()[l, kg * 1408:(kg + 1) * 1408, cb * 128:(cb + 1) * 128]
            kc = 11
        elif kind == "pl":
            _, l, cb = desc
            src = self.w_ple.ap()[l, :, cb * 128:(cb + 1) * 128]
            kc = 2
        src = src.rearrange("(c p) n -> p c n", p=128)
        out = sl[:, 0:kc * 128].rearrange("p (c n) -> p c n", c=kc)
        self.P.dma("pool", lambda e: [e.dma_start(out=out, in_=src)], [], [t_sl], "w%d" % (i % NSLOT))

    def wv(self, desc, kc=8):
        sl, t_sl = self.next_w(desc)
        return sl[:, 0:kc * 128].rearrange("p (c n) -> p c n", c=kc), t_sl

    def next_w(self, desc):
        if self.P.dry:
            self.wsched.append(desc)
            return self.wsl[0]
        i = self.wpos
        assert self.wsched[i] == desc, (self.wsched[i], desc)
        while self.wissued < min(len(self.wsched), i + NSLOT - 1):
            self.issue_w(self.wissued)
            self.wissued += 1
        self.wpos += 1
        return self.wsl[i % NSLOT]

    def rstd(self, chunks, n, ones, t_ones, bank=None):
        ps, t_ps = self.pw() if bank is None else (self.psq[bank], self.t_psq[bank])
        nchunk = len(chunks)
        for c, (ap, tl) in enumerate(chunks):
            sq, t_sq = self.sq[self.sq_i]
            self.sq_i ^= 1
            if c % 2 == 0:
                self.A(sq[:, 0:n], ap, AF.Square, tl, [t_sq])
            else:
                self.TT(sq[:, 0:n], ap, ap, ALU.mult, tl, [t_sq])
            self.MM(ps[:, 0:n], [(ones[:], sq[:, 0:n])], [t_sq, t_ones], [t_ps], start=(c == 0), stop=(c == nchunk - 1))
        rst, t_rst = self.tf["rst"]
        self.A(rst[:, 0:n], ps[:, 0:n], AF.Ln, [t_ps], [t_rst], bias=EPS)
        self.A(rst[:, 0:n], rst[:, 0:n], AF.Exp, [t_rst], [t_rst], scale=-0.5)
        return rst[:, 0:n], t_rst

    def stat_begin(self, bank):
        return {"ps": self.psq[bank], "t": self.t_psq[bank], "c": 0, "pend": None}

    def stat_flush(self, st):
        if st["pend"] is not None:
            st["pend"]()
            st["pend"] = None

    def stat_add(self, st, ap, tl, n, last, eng=None):
        self.stat_flush(st)
        c = st["c"]
        sq, t_sq = self.sq[self.sq_i]
        self.sq_i ^= 1
        if (c % 2 == 0 and eng is None) or eng == "act":
            self.A(sq[:, 0:n], ap, AF.Square, tl, [t_sq])
        else:
            self.TT(sq[:, 0:n], ap, ap, ALU.mult, tl, [t_sq])
        st["pend"] = lambda: self.MM(st["ps"][:, 0:n], [(self.onesD[:], sq[:, 0:n])], [t_sq, self.t_onesD], [st["t"]],
                                     start=(c == 0), stop=last)
        st["c"] = c + 1

    def stat_finish(self, st, n, key):
        self.stat_flush(st)
        rst, t_rst = self.tf[key]
        self.A(rst[:, 0:n], st["ps"][:, 0:n], AF.Ln, [st["t"]], [t_rst], bias=EPS)
        self.A(rst[:, 0:n], rst[:, 0:n], AF.Exp, [t_rst], [t_rst], scale=-0.5)
        return rst[:, 0:n], t_rst

    def norm_to_h(self, n, goff, rst=None):
        if rst is None:
            st = self.stat_begin(2)
            for c in range(NCH):
                self.stat_add(st, self.xT[:, c, 0:n], [self.t_xTc[c]], n, c == NCH - 1)
            rst = self.stat_finish(st, n, "eb")
        rs, t_rs = rst
        for c in range(NCH):
            self.STT(self.hT[:, c, 0:n], self.xT[:, c, 0:n], self.pcol(goff + c), rs, ALU.mult, ALU.mult,
                     [self.t_xTc[c], t_rs, self.t_ptr], [self.t_hT])

    def resid_norm(self, n, goff, st_m, want_x_stats):
        rs, t_rs = self.stat_finish(st_m, n, "rst")
        st_x = self.stat_begin(2) if want_x_stats else None
        for c in range(NCH):
            tmp, t_tmp = self.tf["ocp" if c % 2 else "qs"]
            self.STT(tmp[:, 0:n], self.mT[:, c, 0:n], self.pcol(goff + c), rs, ALU.mult, ALU.mult,
                     [self.t_mTc[c], t_rs, self.t_ptr], [t_tmp])
            self.TT(self.xT[:, c, 0:n], self.xT[:, c, 0:n], tmp[:, 0:n], ALU.add, [self.t_xTc[c], t_tmp], [self.t_xTc[c]])
            if want_x_stats:
                self.stat_add(st_x, self.xT[:, c, 0:n], [self.t_xTc[c]], n, c == NCH - 1)
        if want_x_stats:
            return self.stat_finish(st_x, n, "eb")
        return None

    def proj_to_m(self, n, name, l, src, t_src):
        st = self.stat_begin(3)
        for c in range(8):
            wv, t_sl = self.wv(("sq", name, l, c))
            ps, t_ps = self.pw()
            self.MM(ps[:, 0:n], [(wv[:, k, :], src(k)) for k in range(NCH)], [t_sl] + t_src, [t_ps])
            self.CP("act", self.mT[:, c, 0:n], ps[:, 0:n], [t_ps], [self.t_mTc[c]])
            self.stat_add(st, self.mT[:, c, 0:n], [self.t_mTc[c]], n, c == 7)
        return st

    def run_seq(self, kind, si):
        cfg = self.cfg
        TT_, SEQ, PAST, DEC = cfg.tt, cfg.seq, cfg.past, cfg.dec
        L = SEQ if kind == "p" else DEC
        kbase = 0 if kind == "p" else PAST
        if kind == "p":
            for l in range(NA):
                for h in range(H):
                    self.dve(lambda e, l=l, h=h: e.memset(self.S[:, l, h, :], 0.0), [], [self.t_S[l][h]])
                    self.dve(lambda e, l=l, h=h: e.memset(self.Sb[:, l, h, :], 0.0), [], [self.t_Sb[l][h]])
            for l in range(DEPTH):
                self.dve(lambda e, l=l: e.memset(self.cs[:, l, :, :], 0.0), [], [self.t_cs[l]])
        else:
            for l in range(NA):
                self.DMA("sp", self.S[:, l, :, :], self.sh.ap()[l].rearrange("h k v -> k h v"), [], self.t_S[l], "sin%d" % l)
                for h in range(H):
                    self.CP("dve", self.Sb[:, l, h, :], self.S[:, l, h, :], [self.t_S[l][h]], [self.t_Sb[l][h]])
            for l in range(DEPTH):
                st, t_st = self.ost[self.ost_i]
                self.ost_i ^= 1
                self.DMA("sp", st[0:44, 0:128], self.sc.ap()[l], [], [t_st], "ost%d" % (self.ost_i ^ 1))
                ps, t_ps = self.pq()
                self.TR(ps[:, 0:44], st[0:44, 0:128], 44, [t_st], [t_ps])
                self.CP("dve", self.cs[:, l, :, :].rearrange("p a b -> p (a b)"), ps[:, 0:44], [t_ps], [self.t_cs[l]])
            for t in range(PAST // 128):
                st, t_st = self.ost[self.ost_i]
                oi = self.ost_i
                self.ost_i ^= 1
                self.DMA("sp", st[:], self.ck.ap()[t * 128:(t + 1) * 128, :], [], [t_st], "ost%d" % oi)
                for hh in range(2):
                    ps, t_ps = self.pw()
                    for q in range(4):
                        h = hh * 4 + q
                        self.TR(ps[:, q * 128:(q + 1) * 128], st[:, h * 128:(h + 1) * 128], 128, [t_st], [t_ps])
                    self.CP("dve", self.KT[:, hh * 4:(hh + 1) * 4, t * 128:(t + 1) * 128],
                            ps[:].rearrange("p (a b) -> p a b", a=4), [t_ps], self.t_KT[hh * 4:(hh + 1) * 4])
                st, t_st = self.ost[self.ost_i]
                oi = self.ost_i
                self.ost_i ^= 1
                self.DMA("sp", st[:], self.cv.ap()[t * 128:(t + 1) * 128, :], [], [t_st], "ost%d" % oi)
                self.CP("act", self.Vr[:, t, :], st[:], [t_st], [self.t_Vr[t]])
        for t0 in range(0, L, TT_):
            n = min(TT_, L - t0)
            self.run_tile(kind, si, t0, n, kbase, last=(t0 + n >= L))

    def load_T(self, src_rows, n, dst_fn, nchunk, t_dst, eng="dve"):
        for sub in range((n + 127) // 128):
            m = min(128, n - sub * 128)
            st, t_st = self.ost[self.ost_i]
            oi = self.ost_i
            self.ost_i ^= 1
            self.DMA("sp", st[0:m, 0:nchunk * 128], src_rows(sub, m), [], [t_st], "ost%d" % oi)
            for g0 in range(0, nchunk, 4):
                g = min(4, nchunk - g0)
                ps, t_ps = self.pw()
                for q in range(g):
                    self.TR(ps[:, q * 128:q * 128 + m], st[0:m, (g0 + q) * 128:(g0 + q + 1) * 128], m, [t_st], [t_ps])
                self.CP(eng, dst_fn(g0, g0 + g, sub * 128, m),
                        ps[:, 0:g * 128].rearrange("p (a b) -> p a b", a=g)[:, :, 0:m], [t_ps], t_dst)

    def store_T(self, n, src_fn, t_src, dst_rows, sem_prefix, also=None):
        for sub in range((n + 127) // 128):
            m = min(128, n - sub * 128)
            st, t_st = self.ost[self.ost_i]
            oi = self.ost_i
            self.ost_i ^= 1
            for g0 in (0, 4):
                ps, t_ps = self.pw()
                for q in range(4):
                    c = g0 + q
                    self.TR(ps[0:m, q * 128:(q + 1) * 128], src_fn(c, sub * 128, m), 128, t_src(c), [t_ps])
                self.CP("act" if g0 else "dve", st[0:m, g0 * 128:(g0 + 4) * 128], ps[0:m, :], [t_ps], [t_st])
                if also is not None:
                    also(sub, m, g0, ps, t_ps)
            self.DMA("sp", dst_rows(sub, m), st[0:m, :], [t_st], [], "ost%d" % oi)

    def run_tile(self, kind, si, t0, n, kbase, last):
        cfg = self.cfg
        xsrc = self.xp if kind == "p" else self.xs
        self.load_T(lambda sub, m: xsrc.ap()[si, t0 + sub * 128: t0 + sub * 128 + m, :], n,
                    lambda c0, c1, col, m: self.xT[:, c0:c1, col:col + m], NCH, self.t_xTc)
        xr = None
        for l in range(cfg.nlayers):
            g0 = self.o_gn + l * 32
            self.norm_to_h(n, g0, xr)
            if l < NA:
                self.hgrn(l, n)
                st = self.proj_to_m(n, "w_out_a", l, lambda k: self.aT[:, k, 0:n], self.t_aTc[0:8])
            else:
                self.attn(l, kind, t0, n, kbase)
                st = self.proj_to_m(n, "w_out_b", l - NA, lambda k: self.aT[:, k, 0:n], self.t_aTc[0:8])
            xr = self.resid_norm(n, g0 + 8, st, True)
            self.ple_load(l, kind, si, t0, n)
            self.norm_to_h(n, g0 + 16, xr)
            st = self.ffn(l, n)
            self.resid_norm(n, g0 + 24, st, False)
            xr = self.ple(l, kind, si, t0, n)
            if l == NA - 1:
                self.kv(kind, si, t0, n, kbase, xr)
        ydst = self.y_p if kind == "p" else self.y_s
        self.store_T(n, lambda c, col, m: self.xT[:, c, col:col + m], lambda c: [self.t_xTc[c]],
                     lambda sub, m: ydst.ap()[si, t0 + sub * 128:t0 + sub * 128 + m, :], "y")
        if last:
            hdst = self.hg_p if kind == "p" else self.hg_s
            for l in range(NA):
                self.DMA("sp", hdst.ap()[l, si].rearrange("h k v -> k h v"), self.S[:, l, :, :], self.t_S[l], [], "sout%d" % l)
            cdst = self.cv_p if kind == "p" else self.cv_s
            for l in range(DEPTH):
                ps, t_ps = self.pq()
                self.TR(ps[0:44, :], self.cs[:, l, :, :].rearrange("p a b -> p (a b)"), 128, [self.t_cs[l]], [t_ps])
                st, t_st = self.ost[self.ost_i]
                oi = self.ost_i
                self.ost_i ^= 1
                self.CP("dve", st[0:44, 0:128], ps[0:44, :], [t_ps], [t_st])
                self.P.dma("sp", lambda e, l=l, st=st: [
                    e.dma_start(out=cdst.ap()[l, si, r, :].rearrange("(j p) -> j p", p=128), in_=st[r * 22:(r + 1) * 22, 0:128])
                    for r in range(2)], [t_st], [], "ost%d" % oi, nd=2)

    def hgrn(self, l, n):
        tf = self.tf
        nsub = (n + 127) // 128
        C = 64 if n >= 64 else n
        nchk = n // C
        s, t_s = tf["s"]
        lf, t_lf = tf["lf"]
        b, t_b = tf["b"]
        eb, t_eb = tf["eb"]
        enb, t_enb = tf["enb"]
        qs, t_qs = tf["qs"]
        ktf, t_ktf = tf["ktf"]
        kdf, t_kdf = tf["b"]
        vTf, t_vTf = self.vTfb
        vtok, t_vtok = self.vtok[0]
        kdtok, t_kdtok = self.kdtok[0]
        Am, t_Am = self.Am[0]

        def bufs(h):
            par = h % 2
            return ((self.u8[:, 0 + par, :], self.t_u8[0 + par]), (self.u8[:, 2 + par, :], self.t_u8[2 + par]),
                    (self.u8[:, 4 + par, :], self.t_u8[4 + par]), self.ebl[par])

        def Pg(h, key):
            g_ = {"q": 0, "f": 1, "i": 2, "g": 3}[key]
            wv, t_sl = self.wv(("in", l, h, g_))
            ps, t_ps = self.pw()
            self.MM(ps[:, 0:n], [(wv[:, k, :], self.hT[:, k, 0:n]) for k in range(NCH)], [t_sl, self.t_hT], [t_ps])
            return ps, t_ps

        def E1a(h, bf, bq):
            (psf, t_psf), (psq_, t_psq_) = bf, bq
            self.A(s[:, 0:n], psf[:, 0:n], AF.Sigmoid, [t_psf], [t_s], scale=-1.0)
            self.A(qs[:, 0:n], psq_[:, 0:n], AF.Silu, [t_psq_], [t_qs])

        def E1b(h, bi, bg):
            _, _, (gate, t_gate), _ = bufs(h)
            (psi, t_psi), (psg, t_psg) = bi, bg
            self.A(gate[:, 0:n], psg[:, 0:n], AF.Silu, [t_psg], [t_gate])
            self.CP("dve", vTf[:, 0:n], psi[:, 0:n], [t_psi], [t_vTf])

        def E2steps(h):
            (qt, t_qt), (kt, t_kt), (gate, t_gate), (ebl, t_ebl) = bufs(h)
            ebv = eb[:, 0:n].rearrange("p (c t) -> p c t", t=C)[:, :, C - 1:C]
            return [
                lambda: self.A(lf[:, 0:n], s[:, 0:n], AF.Ln, [t_s, self.t_noml], [t_lf],
                               scale=self.noml[:, l, h:h + 1], bias=1.0),
                lambda: self.dve(lambda e: e.tensor_tensor_scan(out=b[:, 0:n], data0=self.smask[:, 0:n], data1=lf[:, 0:n],
                                                                initial=0.0, op0=ALU.mult, op1=ALU.add),
                                 [t_lf, self.t_smask], [t_b]),
                lambda: (self.A(eb[:, 0:n], b[:, 0:n], AF.Exp, [t_b], [t_eb]),
                         self.A(enb[:, 0:n], b[:, 0:n], AF.Exp, [t_b], [t_enb], scale=-1.0)),
                lambda: self.TT(qt[:, 0:n], qs[:, 0:n], eb[:, 0:n], ALU.mult, [t_qs, t_eb], [t_qt]),
                lambda: self.STT(ktf[:, 0:n], s[:, 0:n], self.oml[:, l, h:h + 1], enb[:, 0:n], ALU.mult, ALU.mult,
                                 [t_s, t_enb, self.t_oml], [t_ktf]),
                lambda: (self.CP("dve", kt[:, 0:n], ktf[:, 0:n], [t_ktf], [t_kt]),
                         self.CP("dve", ebl[:, 0:nchk].unsqueeze(2), ebv, [t_eb], [t_ebl])),
                lambda: self.TT(kdf[:, 0:n].rearrange("p (c t) -> p c t", t=C),
                                ktf[:, 0:n].rearrange("p (c t) -> p c t", t=C),
                                ebv.to_broadcast([128, nchk, C]), ALU.mult, [t_ktf, t_eb], [t_kdf]),
            ]

        def E2(h):
            for f_ in E2steps(h):
                f_()

        def T(h):
            (qt, t_qt), (kt, t_kt), _, _ = bufs(h)
            for sub in range(nsub):
                m = min(128, n - sub * 128)
                ps, t_ps = self.pq()
                self.TR(ps[0:m, :], vTf[:, sub * 128:sub * 128 + m], 128, [t_vTf], [t_ps])
                self.CP("dve", vtok[0:m, sub, :], ps[0:m, :], [t_ps], [t_vtok])
                ps, t_ps = self.pq()
                self.TR(ps[0:m, :], kdf[:, sub * 128:sub * 128 + m], 128, [t_kdf], [t_ps])
                self.CP("act", kdtok[0:m, sub, :], ps[0:m, :], [t_ps], [t_kdtok])
                ps, t_ps = self.pq()
                self.MM(ps[0:m, 0:m], [(kt[:, sub * 128:sub * 128 + m], qt[:, sub * 128:sub * 128 + m])], [t_kt, t_qt], [t_ps])
                self.TT(Am[0:m, sub, 0:m], ps[0:m, 0:m], self.tri[0:m, 0:m], ALU.mult, [t_ps, self.t_tri], [t_Am])

        def G(h, filler=None):
            (qt, t_qt), _, _, (ebl, t_ebl) = bufs(h)
            pso, t_pso = self.psq[2], self.t_psq[2]
            t_S, t_Sb = self.t_S[l][h], self.t_Sb[l][h]
            for c in range(nchk):
                sub, off = (c * C) // 128, (c * C) % 128
                m = min(128, n - sub * 128)
                cols = slice(c * C, (c + 1) * C)
                self.MM(pso[:, cols], [(self.Sb[:, l, h, :], qt[:, cols]),
                                       (vtok[0:m, sub, :], Am[0:m, sub, off:off + C])],
                        [t_Sb, t_qt, t_vtok, t_Am], [t_pso])
                ps, t_ps = self.pq()
                self.MM(ps, [(kdtok[off:off + C, sub, :], vtok[off:off + C, sub, :])], [t_kdtok, t_vtok], [t_ps])
                self.STT(self.Sb[:, l, h, :], self.S[:, l, h, :], ebl[:, c:c + 1], ps, ALU.mult, ALU.add,
                         [t_S, t_ebl, t_ps], [t_Sb])
                self.STT(self.S[:, l, h, :], self.S[:, l, h, :], ebl[:, c:c + 1], ps, ALU.mult, ALU.add,
                         [t_S, t_ebl, t_ps], [t_S])
                if filler is not None:
                    filler(c)
            return pso, t_pso

        def N(h, pso, t_pso):
            _, _, (gate, t_gate), _ = bufs(h)
            ocp, t_ocp = tf["ocp"]
            self.CP("act", ocp[:, 0:n], pso[:, 0:n], [t_pso], [t_ocp])
            rst, t_rst = self.rstd([(ocp[:, 0:n], [t_ocp])], n, self.onesH, self.t_onesH, bank=2)
            self.STT(ocp[:, 0:n], ocp[:, 0:n], self.pcol(self.o_ghg + l * 8 + h), rst, ALU.mult, ALU.mult,
                     [t_ocp, t_rst, self.t_ptr], [t_ocp])
            self.TT(self.aT[:, h, 0:n], ocp[:, 0:n], gate[:, 0:n], ALU.mult, [t_ocp, t_gate], [self.t_aTc[h]])

        self.pq_n = 3
        bf, bq = Pg(0, "f"), Pg(0, "q")
        bi, bg = Pg(0, "i"), Pg(0, "g")
        E1a(0, bf, bq)
        E1b(0, bi, bg)
        E2(0)
        per = -(-16 // nchk)
        for h in range(H):
            nxt = h + 1 < H
            if nxt:
                bf, bq = Pg(h + 1, "f"), Pg(h + 1, "q")
            T(h)
            filler = None
            if nxt:
                E1a(h + 1, bf, bq)
                st_ = {"pos": 0, "banks": {}, "wv": {}}

                def filler(c, st_=st_, h=h):
                    todo = per if c < nchk - 1 else 16 - st_["pos"]
                    while todo > 0 and st_["pos"] < 16:
                        key = "i" if st_["pos"] < 8 else "g"
                        k0 = st_["pos"] % 8
                        if k0 == 0:
                            st_["wv"][key] = self.wv(("in", l, h + 1, 2 if key == "i" else 3))
                            st_["banks"][key] = self.pw()
                        k1 = min(8, k0 + todo)
                        wv, t_sl = st_["wv"][key]
                        ps, t_ps = st_["banks"][key]
                        self.MM(ps[:, 0:n], [(wv[:, k, :], self.hT[:, k, 0:n]) for k in range(k0, k1)],
                                [t_sl, self.t_hT], [t_ps], start=(k0 == 0), stop=(k1 == 8))
                        todo -= k1 - k0
                        st_["pos"] += k1 - k0
            hook = filler
            if nxt:
                steps = E2steps(h + 1)
                sp = {"i": 0}

                def hook(c, filler=filler, steps=steps, sp=sp):
                    filler(c)
                    upto = len(steps) if c == nchk - 1 else -(-(c + 1) * len(steps) // nchk)
                    while sp["i"] < upto:
                        steps[sp["i"]]()
                        sp["i"] += 1
            pso, t_pso = G(h, hook)
            N(h, pso, t_pso)
            if nxt:
                E1b(h + 1, st_["banks"]["i"], st_["banks"]["g"])
        self.pq_n = 4

    def ffn(self, l, n):
        tf = self.tf
        cw = self.o_cw + l * 3 * NFF
        cb = self.o_cb + l * NFF
        st = self.stat_begin(3)
        Gs = [tf["s"], tf["qs"]]
        accs = [tf["lf"], tf["enb"]]
        sgs = [tf["b"], tf["ktf"]]

        def stage_a(j):
            G, t_G = Gs[j % 2]
            acc, t_acc = accs[j % 2]
            wv, t_sl = self.wv(("fi", l, j, 0))
            psg, t_psg = self.pw()
            self.MM(psg[:, 0:n], [(wv[:, k, :], self.hT[:, k, 0:n]) for k in range(NCH)], [t_sl, self.t_hT], [t_psg])
            wv, t_sl = self.wv(("fi", l, j, 1))
            psu, t_psu = self.pw()
            self.MM(psu[:, 0:n], [(wv[:, k, :], self.hT[:, k, 0:n]) for k in range(NCH)], [t_sl, self.t_hT], [t_psu])
            self.CP("dve", G[:, 0:2], self.cs[:, l, :, j], [self.t_cs[l]], [t_G])
            self.CP("act", G[:, 2:2 + n], psg[:, 0:n], [t_psg], [t_G])
            self.A(acc[:, 0:n], psg[:, 0:n], AF.Identity, [t_psg, self.t_ptr], [t_acc],
                   scale=self.pcol(cw + 2 * NFF + j), bias=self.pcol(cb + j))
            self.CP("dve", self.cs[:, l, :, j], G[:, n:n + 2], [t_G], [self.t_cs[l]])
            self.STT(acc[:, 0:n], G[:, 1:1 + n], self.pcol(cw + NFF + j), acc[:, 0:n], ALU.mult, ALU.add,
                     [t_G, t_acc, self.t_ptr], [t_acc])
            self.STT(acc[:, 0:n], G[:, 0:n], self.pcol(cw + j), acc[:, 0:n], ALU.mult, ALU.add,
                     [t_G, t_acc, self.t_ptr], [t_acc])
            return psu, t_psu

        def stage_b(j, jj, psu, t_psu):
            acc, t_acc = accs[j % 2]
            sg, t_sg = sgs[j % 2]
            self.A(sg[:, 0:n], acc[:, 0:n], AF.Silu, [t_acc], [t_sg])
            self.TT(self.aT[:, jj, 0:n], sg[:, 0:n], psu[:, 0:n], ALU.mult, [t_sg, t_psu], [self.t_aTc[jj]])

        for kg in range(2):
            prev = None
            for jj in range(11):
                j = kg * 11 + jj
                cur = (j, jj) + stage_a(j)
                if prev is not None:
                    stage_b(*prev)
                prev = cur
            stage_b(*prev)
            for c in range(8):
                wv, t_sl = self.wv(("fo", l, kg, c), kc=11)
                ps, t_ps = self.pw()
                self.MM(ps[:, 0:n], [(wv[:, k, :], self.aT[:, k, 0:n]) for k in range(11)], [t_sl] + self.t_aTc, [t_ps])
                if kg == 0:
                    self.CP("act", self.mT[:, c, 0:n], ps[:, 0:n], [t_ps], [self.t_mTc[c]])
                else:
                    self.TT(self.mT[:, c, 0:n], self.mT[:, c, 0:n], ps[:, 0:n], ALU.add,
                            [self.t_mTc[c], t_ps], [self.t_mTc[c]])
                    self.stat_add(st, self.mT[:, c, 0:n], [self.t_mTc[c]], n, c == 7)
        return st

    def ple_load(self, l, kind, si, t0, n):
        psrc = self.pp if kind == "p" else self.psm
        self.load_T(lambda sub, m: psrc.ap()[l, si, t0 + sub * 128:t0 + sub * 128 + m, :], n,
                    lambda c0, c1, col, m: self.plT[:, c0:c1, col:col + m], 2, [self.t_plT], eng="act")

    def ple(self, l, kind, si, t0, n):
        for c in range(NCH):
            self.CP("act" if c % 2 else "dve", self.hT[:, c, 0:n], self.xT[:, c, 0:n], [self.t_xTc[c]], [self.t_hT])
        st = self.stat_begin(2)
        sg, t_sg = self.tf["enb"]
        for c in range(8):
            wp, t_slp = self.wv(("pl", l, c), kc=2)
            wv, t_sl = self.wv(("sq", "w_ple_gate", l, c))
            psg, t_psg = self.pw()
            self.MM(psg[:, 0:n], [(wv[:, k, :], self.hT[:, k, 0:n]) for k in range(NCH)], [t_sl, self.t_hT], [t_psg])
            psp, t_psp = self.pw()
            self.MM(psp[:, 0:n], [(wp[:, k, :], self.plT[:, k, 0:n]) for k in range(2)], [t_slp, self.t_plT], [t_psp])
            self.A(sg[:, 0:n], psg[:, 0:n], AF.Sigmoid, [t_psg], [t_sg])
            self.TT(sg[:, 0:n], sg[:, 0:n], psp[:, 0:n], ALU.mult, [t_sg, t_psp], [t_sg])
            self.TT(self.xT[:, c, 0:n], self.xT[:, c, 0:n], sg[:, 0:n], ALU.add, [self.t_xTc[c], t_sg], [self.t_xTc[c]])
            self.stat_add(st, self.xT[:, c, 0:n], [self.t_xTc[c]], n, c == 7, eng="dve")
        return self.stat_finish(st, n, "eb")

    def kv(self, kind, si, t0, n, kbase, rst):
        self.norm_to_h(n, self.o_gkv, rst)
        kpos = kbase + t0
        for cb in range(16):
            wv, t_sl = self.wv(("kv", cb))
            c = cb % 8
            ps, t_ps = self.pw()
            self.MM(ps[:, 0:n], [(wv[:, k, :], self.hT[:, k, 0:n]) for k in range(NCH)], [t_sl, self.t_hT], [t_ps])
            self.CP("act", self.mT[:, c, 0:n], ps[:, 0:n], [t_ps], [self.t_mTc[c]])
            if cb < 8:
                self.CP("act", self.KT[:, c, kpos:kpos + n], ps[:, 0:n], [t_ps], [self.t_KT[c]])
            if cb == 7:
                kdst = self.k_p if kind == "p" else self.k_s
                self.store_T(n, lambda c, col, m: self.mT[:, c, col:col + m], lambda c: [self.t_mTc[c]],
                             lambda sub, m: kdst.ap()[si, t0 + sub * 128:t0 + sub * 128 + m, :], "k")
            if cb == 15:
                vdst = self.v_p if kind == "p" else self.v_s

                def also(sub, m, g0, ps, t_ps):
                    kt_ = (kpos + sub * 128) // 128
                    po = (kpos + sub * 128) % 128
                    self.CP("dve" if g0 else "act", self.Vr[po:po + m, kt_, g0 * 128:(g0 + 4) * 128], ps[0:m, :],
                            [t_ps], [self.t_Vr[kt_]])
                self.store_T(n, lambda c, col, m: self.mT[:, c, col:col + m], lambda c: [self.t_mTc[c]],
                             lambda sub, m: vdst.ap()[si, t0 + sub * 128:t0 + sub * 128 + m, :], "v", also=also)

    def attn(self, l, kind, t0, n, kbase):
        j = l - NA
        tf = self.tf
        for c in range(8):
            wv, t_sl = self.wv(("sq", "w_q_b", j, c))
            ps, t_ps = self.pw()
            self.MM(ps[:, 0:n], [(wv[:, k, :], self.hT[:, k, 0:n]) for k in range(NCH)], [t_sl, self.t_hT], [t_ps])
            self.CP("act" if c % 2 else "dve", self.u8[:, c, 0:n], ps[:, 0:n], [t_ps], [self.t_u8[c]])
        nsub = (n + 127) // 128
        qpos0 = kbase + t0
        kend = qpos0 + n
        ktiles = [(kt_, min(128, kend - kt_ * 128)) for kt_ in range((kend + 127) // 128)]
        nk = len(ktiles)
        flat = lambda t: t[:].rearrange("p a b -> p (a b)")
        pbs = [(flat(self.vtok[0][0]), self.vtok[0][1]), (flat(self.kdtok[0][0]), self.kdtok[0][1]),
               (flat(self.Am[0][0]), self.Am[0][1]), (self.plT[:, 0, :], self.t_plT)]
        (pv0, t_pv0), (pv1, t_pv1), (z0, t_z0), (z1, t_z1) = [(self.psq[q], self.t_psq[q]) for q in range(4)]
        scts = [tf["s"], tf["lf"]]
        for h in range(H):
            def scores(ki):
                kt_, kn = ktiles[ki]
                delta = kt_ * 128 - qpos0
                qa = max(0, delta)
                s0 = qa // 128
                s_far = min(nsub, max(s0, (delta + 384) // 128))
                c_far = min(n, s_far * 128)
                for cmap in range(2):
                    bi = (ki % 2) * 2 + cmap
                    sc, t_sc = self.psw[bi], self.t_psw[bi]
                    rows = slice(cmap * 64, (cmap + 1) * 64)
                    self.MM(sc[0:kn, qa:n], [(self.KT[rows, h, kt_ * 128:kt_ * 128 + kn], self.u8[rows, h, qa:n])],
                            [self.t_KT[h], self.t_u8[h]], [t_sc])
                for cmap in range(2):
                    bi = (ki % 2) * 2 + cmap
                    sc, t_sc = self.psw[bi], self.t_psw[bi]
                    Pb, t_Pb = pbs[bi]
                    if n > c_far:
                        self.A(Pb[0:kn, c_far:n], sc[0:kn, c_far:n], AF.Exp, [t_sc, self.t_farb], [t_Pb],
                               scale=0.125, bias=self.farb[0:kn, h:h + 1])
                    if c_far > qa:
                        sct, t_sct = scts[cmap]
                        for s_ in range(s0, s_far):
                            m = min(128, n - s_ * 128)
                            vi = {0: 0, -128: 1, -256: 2}[delta - 128 * s_]
                            cs_ = slice(s_ * 128, s_ * 128 + m)
                            self.STT(sct[0:kn, cs_], sc[0:kn, cs_], 0.125, self.TB[0:kn, vi, h, 0:m], ALU.mult, ALU.add,
                                     [t_sc, self.t_TB], [t_sct])
                        self.A(Pb[0:kn, qa:c_far], sct[0:kn, qa:c_far], AF.Exp, [t_sct], [t_Pb])

            def pvz(ki):
                kt_, kn = ktiles[ki]
                qa = max(0, kt_ * 128 - qpos0)
                st = (ki == 0)
                for cmap, (pv, t_pv, z, t_z) in enumerate(((pv0, t_pv0, z0, t_z0), (pv1, t_pv1, z1, t_z1))):
                    Pb, t_Pb = pbs[(ki % 2) * 2 + cmap]
                    self.MM(pv[:, qa:n], [(self.Vr[0:kn, kt_, h * 128:(h + 1) * 128], Pb[0:kn, qa:n])],
                            [self.t_Vr[kt_], t_Pb], [t_pv], start=st, stop=False, skip=True)
                    self.MM(z[:, qa:n], [(self.ones[0:kn, :], Pb[0:kn, qa:n])], [self.t_ones, t_Pb], [t_z],
                            start=st, stop=False, skip=True)

            scores(0)
            for ki in range(nk):
                if ki + 1 < nk:
                    scores(ki + 1)
                pvz(ki)
            rz0, t_rz0 = tf["b"]
            rz1, t_rz1 = tf["enb"]
            t1, t_t1 = tf["qs"]
            o0, t_o0 = tf["ktf"]
            ocp, t_ocp = tf["ocp"]
            self.A(rz0[:, 0:n], z0[:, 0:n], AF.Ln, [t_z0], [t_rz0])
            self.A(rz1[:, 0:n], z1[:, 0:n], AF.Ln, [t_z1], [t_rz1])
            self.A(rz0[:, 0:n], rz0[:, 0:n], AF.Exp, [t_rz0], [t_rz0], scale=-1.0)
            self.A(rz1[:, 0:n], rz1[:, 0:n], AF.Exp, [t_rz1], [t_rz1], scale=-1.0)
            self.STT(t1[:, 0:n], pv1[:, 0:n], self.nlam[:, j:j + 1], rz1[:, 0:n], ALU.mult, ALU.mult,
                     [t_pv1, t_rz1, self.t_nlam], [t_t1])
            self.TT(o0[:, 0:n], pv0[:, 0:n], rz0[:, 0:n], ALU.mult, [t_pv0, t_rz0], [t_o0])
            self.TT(ocp[:, 0:n], o0[:, 0:n], t1[:, 0:n], ALU.add, [t_o0, t_t1], [t_ocp])
            rst, t_rst = self.rstd([(ocp[:, 0:n], [t_ocp])], n, self.onesH, self.t_onesH)
            self.STT(self.aT[:, h, 0:n], ocp[:, 0:n], self.gs2[:, j, h:h + 1], rst, ALU.mult, ALU.mult,
                     [t_ocp, t_rst, self.t_gs2], [self.t_aTc[h]])


def pack_ptab(g_norms, lb_raw, g_hg, g_kv, g_subln, conv_w, conv_b):
    rows = [g_norms.reshape(-1, 128), lb_raw.reshape(-1, 128), g_hg.reshape(-1, 128), g_kv.reshape(-1, 128),
            g_subln.reshape(-1, 128), conv_w.reshape(-1, 128), conv_b.reshape(-1, 128)]
    t = np.concatenate(rows, axis=0).astype(np.float32)
    out = np.zeros((640, 128), np.float32)
    out[:t.shape[0]] = t
    return out


def make_in_maps(cfg, ncores, x_prompt, x_sample, cache_k, cache_v, state_hgrn, state_conv, p_prompt, p_sample,
                 g_norms, w_in_a, lb_raw, g_hg, w_out_a, g_kv, w_kv, rel_bias, w_q_b, lam_b, g_subln, w_out_b,
                 w_ffn_in, conv_w, conv_b, w_ffn_out, w_ple, w_ple_gate):
    f = lambda a: np.ascontiguousarray(np.asarray(a, dtype=np.float32))
    cst = _consts(cfg)
    shared = dict(
        ptab=pack_ptab(f(g_norms), f(lb_raw), f(g_hg), f(g_kv), f(g_subln), f(conv_w), f(conv_b)),
        rb=f(rel_bias), lamb=f(lam_b).reshape(1, 512),
        w_in_a=f(w_in_a), w_out_a=f(w_out_a), w_kv=f(w_kv), w_q_b=f(w_q_b), w_out_b=f(w_out_b),
        w_ffn_in=f(w_ffn_in), w_ffn_out=f(w_ffn_out), w_ple=f(w_ple), w_ple_gate=f(w_ple_gate),
        c_ident=cst["ident"], c_e1=cst["e1"], c_tri=cst["tri"], c_smask=cst["smask"])
    NP = cfg.nprompt
    maps = []
    for c in range(ncores):
        m = dict(shared)
        m["xp"] = f(x_prompt[c * NP:(c + 1) * NP])
        m["xs"] = f(x_sample[c:c + 1])
        m["ck"] = f(cache_k[c]).reshape(cfg.past, D)
        m["cv"] = f(cache_v[c]).reshape(cfg.past, D)
        m["sh"] = f(state_hgrn[:, c])
        m["sc"] = f(state_conv[:, c]).reshape(DEPTH, 2 * NFF, 128)
        m["pp"] = f(p_prompt[:, c * NP:(c + 1) * NP])
        m["psm"] = f(p_sample[:, c:c + 1])
        maps.append(m)
    return maps


def gather(cfg, res, ncores):
    cat = lambda k, ax: np.concatenate([np.asarray(r[k]) for r in res], axis=ax)
    NP, SEQ, DEC = cfg.nprompt, cfg.seq, cfg.dec
    y_p = cat("y_p", 0)
    y_s = cat("y_s", 0)
    k_p = cat("k_p", 0).reshape(ncores * NP, SEQ, H, 128)
    v_p = cat("v_p", 0).reshape(ncores * NP, SEQ, H, 128)
    k_s = cat("k_s", 0).reshape(ncores, DEC, H, 128)
    v_s = cat("v_s", 0).reshape(ncores, DEC, H, 128)
    hg_p = cat("hg_p", 1)
    hg_s = cat("hg_s", 1)
    cv_p = cat("cv_p", 1)
    cv_s = cat("cv_s", 1)
    return tuple(np.ascontiguousarray(a, dtype=np.float32)
                 for a in (y_p, y_s, k_p, v_p, k_s, v_s, hg_p, hg_s, cv_p, cv_s))


def kernel(**inputs):
    cfg = Cfg()
    ncores = 8
    nc = KB(cfg).build()
    maps = make_in_maps(cfg, ncores, **inputs)
    res = run_bass_kernel_spmd(nc, maps, core_ids=list(range(ncores)))
    return gather(cfg, res.results, ncores)
```

```python
import math
from contextlib import ExitStack

import numpy as np
import concourse.bass as bass
import concourse.mybir as mybir
from concourse.bass_utils import run_bass_kernel_spmd

F32 = mybir.dt.float32
BF16 = mybir.dt.bfloat16
AF = mybir.ActivationFunctionType
ALU = mybir.AluOpType
AX = mybir.AxisListType

D = 1024
NCH = 8
DFF = 2816
NFF = 22
PLE = 256
H = 8
DEPTH = 4
NA = 2
EPS = 1e-6
NEG = -30000.0
ENGS = ("pe", "act", "dve", "pool", "sp")
NSLOT = 6


class Cfg:
    def __init__(self, seq=2048, tt=512, past=1024, dec=32, nprompt=2):
        self.seq, self.tt, self.past, self.dec, self.nprompt = seq, tt, past, dec, nprompt
        self.nlayers, self.stop = DEPTH, None
        self.kmax = max(seq, past + 128)


class Tl:
    __slots__ = ("name", "w", "r", "psum")

    def __init__(self, name, psum=False):
        self.name, self.w, self.r, self.psum = name, None, {}, psum


class Op:
    __slots__ = ("eng", "idx", "fn", "waits", "dwaits", "signal", "seq", "dma", "dsem", "dcum")


class Prog:
    def __init__(self, nc, es):
        self.nc, self.es = nc, es
        self.ops = {e: [] for e in ENGS}
        self.waited = {e: {f: -1 for f in ENGS} for e in ENGS}
        self.dwaited = {e: {} for e in ENGS}
        self.dsems = {}
        self.ndma = 0
        self.dry = False

    def dsem(self, name):
        if name not in self.dsems:
            self.dsems[name] = [self.es.enter_context(self.nc.semaphore("d_" + name)), 0]
        return name

    def _dep(self, o, d):
        if d is None:
            return
        if d.dma:
            if self.dwaited[o.eng].get(d.dsem, 0) >= d.dcum:
                return
            o.dwaits[d.dsem] = max(o.dwaits.get(d.dsem, 0), d.dcum)
            self.dwaited[o.eng][d.dsem] = d.dcum
            return
        if d.eng == o.eng:
            if o.eng == "pe":
                return
        if self.waited[o.eng][d.eng] >= d.idx:
            return
        o.waits[d.eng] = max(o.waits.get(d.eng, -1), d.idx)
        self.waited[o.eng][d.eng] = d.idx
        d.signal = True

    def _rec(self, eng, fn, r, w, dma=False, dsem=None, nd=1):
        if self.dry:
            return None
        o = Op()
        o.eng, o.fn, o.waits, o.dwaits, o.signal, o.seq = eng, fn, {}, {}, False, 0
        o.dma, o.dsem, o.dcum = dma, dsem, 0
        o.idx = len(self.ops[eng])
        for t in r:
            self._dep(o, t.w)
            if t.psum:
                for k, d in t.r.items():
                    if k != eng:
                        self._dep(o, d)
        for t in w:
            self._dep(o, t.w)
            for d in t.r.values():
                self._dep(o, d)
        if dma:
            self.dsems[dsem][1] += 16 * nd
            o.dcum = self.dsems[dsem][1]
            self.ndma += 1
        for t in r:
            t.r[("dma", self.ndma) if dma else eng] = o
        for t in w:
            t.w, t.r = o, {}
        self.ops[eng].append(o)
        return o

    def op(self, eng, fn, r=(), w=()):
        return self._rec(eng, fn, r, w)

    def dma(self, q, fn, r, w, sem, nd=1):
        return self._rec(q, fn, r, w, dma=True, dsem=self.dsem(sem), nd=nd)

    def emit(self):
        nc, es = self.nc, self.es
        sems = {e: es.enter_context(nc.semaphore("e_" + e)) for e in ("pe", "act", "dve", "pool")}
        for e in ENGS:
            c = 0
            for o in self.ops[e]:
                if o.signal and not o.dma:
                    c += 1
                    o.seq = c
        block = es.enter_context(nc.Block())
        ops, dsems = self.ops, self.dsems

        def run(e, eo):
            for o in ops[e]:
                for f, idx in o.waits.items():
                    eo.wait_ge(sems[f], ops[f][idx].seq)
                for key, val in o.dwaits.items():
                    eo.wait_ge(dsems[key][0], val)
                ins = o.fn(eo)
                if o.dma:
                    for i in ins:
                        i.then_inc(dsems[o.dsem][0], 16)
                elif o.signal:
                    ins.then_inc(sems[e], 1)
            if e == "sp":
                for key, (h, cum) in dsems.items():
                    if cum:
                        eo.wait_ge(h, cum)

        block.tensor(lambda t: run("pe", t))
        block.scalar(lambda a: run("act", a))
        block.vector(lambda v: run("dve", v))
        block.gpsimd(lambda g: run("pool", g))
        block.sync(lambda s: run("sp", s))


def _t5_bucket_np(rel):
    rel = np.asarray(rel, np.int64)
    half = 16
    ret = np.where(rel > 0, half, 0)
    n = np.abs(rel)
    max_exact = 8
    nf = np.maximum(n, 1).astype(np.float32)
    large = max_exact + (np.log(nf / np.float32(max_exact)) / np.float32(math.log(256 / max_exact))
                         * np.float32(half - max_exact)).astype(np.int32)
    large = np.minimum(large, half - 1)
    return ret + np.where(n < max_exact, n, large)


def _consts(cfg):
    c = {}
    c["ident"] = np.eye(128, dtype=np.float32)
    e1 = np.zeros((32, 512), np.float32)
    b = _t5_bucket_np(127 - np.arange(512))
    e1[b, np.arange(512)] = 1.0
    c["e1"] = e1
    m = np.arange(128)[:, None]
    l = np.arange(128)[None, :]
    c["tri"] = ((m // 64 == l // 64) & (l >= m)).astype(np.float32)
    sm = np.ones((128, 512), np.float32)
    sm[:, ::64] = 0.0
    c["smask"] = sm
    return c


class KB:
    def __init__(self, cfg):
        self.cfg = cfg
        self.nc = bass.Bass("TRN2", target_bir_lowering=False)
        self.es = ExitStack()
        self.P = Prog(self.nc, self.es)
        self.wq = []
        self.wpos = 0
        self.wissued = 0

    def din(self, name, shape, dt=F32):
        return self.nc.dram_tensor(name, list(shape), dt, kind="ExternalInput")

    def dout(self, name, shape, dt=F32):
        return self.nc.dram_tensor(name, list(shape), dt, kind="ExternalOutput")

    def sb(self, name, shape, dt=F32):
        t = self.es.enter_context(self.nc.sbuf_tensor(name, list(shape), dt))
        return t, Tl(name)

    def pe(self, fn, r, w):
        return self.P.op("pe", fn, r, w)

    def act(self, fn, r, w):
        return self.P.op("act", fn, r, w)

    def dve(self, fn, r, w):
        return self.P.op("dve", fn, r, w)

    def A(self, out, in_, func, r, w, bias=None, scale=None):
        kw = {}
        if bias is not None:
            kw["bias"] = bias
        if scale is not None:
            kw["scale"] = scale
        return self.act(lambda e: e.activation(out=out, in_=in_, func=func, **kw), r, w)

    def TT(self, out, a, b, op, r, w):
        return self.dve(lambda e: e.tensor_tensor(out=out, in0=a, in1=b, op=op), r, w)

    def STT(self, out, in0, scalar, in1, op0, op1, r, w):
        return self.dve(lambda e: e.scalar_tensor_tensor(out=out, in0=in0, scalar=scalar, in1=in1,
                                                         op0=op0, op1=op1), r, w)

    def TS(self, out, in0, s1, s2, op0, op1, r, w):
        if s2 is None:
            return self.dve(lambda e: e.tensor_scalar(out=out, in0=in0, scalar1=s1, scalar2=None, op0=op0), r, w)
        return self.dve(lambda e: e.tensor_scalar(out=out, in0=in0, scalar1=s1, scalar2=s2,
                                                  op0=op0, op1=op1), r, w)

    def CP(self, eng, out, in_, r, w):
        if eng == "act":
            return self.act(lambda e: e.copy(out=out, in_=in_), r, w)
        return self.P.op(eng, lambda e: e.tensor_copy(out=out, in_=in_), r, w)

    def MM(self, out, pairs, r, w, start=True, stop=True, skip=False):
        def fn(e):
            n = len(pairs)
            ins = None
            for i, (l, rh) in enumerate(pairs):
                if skip:
                    ins = e.matmul(out, lhsT=l, rhs=rh, start=(start and i == 0), stop=(stop and i == n - 1),
                                   skip_group_check=True)
                else:
                    ins = e.matmul(out, lhsT=l, rhs=rh, start=(start and i == 0), stop=(stop and i == n - 1))
            return ins
        return self.pe(fn, r, w)

    def MMh(self, ps, wv, t_sl, t_ps, n, split=False):
        if not split:
            return self.MM(ps[:, 0:n], [(wv[:, k, :], self.hT[:, k, 0:n]) for k in range(NCH)],
                           [t_sl] + self.t_hTc, [t_ps])
        for k in range(NCH):
            self.MM(ps[:, 0:n], [(wv[:, k, :], self.hT[:, k, 0:n])], [t_sl, self.t_hTc[k]], [t_ps],
                    start=(k == 0), stop=(k == NCH - 1))

    def TR(self, out, in_, k, r, w):
        ident = self.ident
        return self.pe(lambda e: e.transpose(out, in_, ident[0:k, 0:k]), r + [self.t_ident], w)

    def DMA(self, q, out, in_, r, w, sem):
        return self.P.dma(q, lambda e: [e.dma_start(out=out, in_=in_)], r, w, sem)

    def pw(self):
        i = self.pw_i
        self.pw_i = (i + 1) % 4
        return self.psw[i], self.t_psw[i]

    def pq(self):
        i = self.pq_i % self.pq_n
        self.pq_i = (i + 1) % self.pq_n
        if self.pq_n == 3:
            i = (0, 1, 3)[i]
        return self.psq[i][:, 0:128], self.t_psq[i]

    def build(self):
        cfg, nc = self.cfg, self.nc
        TT_, SEQ, PAST, DEC, NP = cfg.tt, cfg.seq, cfg.past, cfg.dec, cfg.nprompt
        KMAX = cfg.kmax
        self.xp = self.din("xp", [NP, SEQ, D])
        self.xs = self.din("xs", [1, DEC, D])
        self.ck = self.din("ck", [PAST, D])
        self.cv = self.din("cv", [PAST, D])
        self.sh = self.din("sh", [NA, H, 128, 128])
        self.sc = self.din("sc", [DEPTH, 2 * NFF, 128])
        self.pp = self.din("pp", [DEPTH, NP, SEQ, PLE])
        self.psm = self.din("psm", [DEPTH, 1, DEC, PLE])
        self.ptab = self.din("ptab", [640, 128])
        self.rb = self.din("rb", [32, H])
        self.lamb = self.din("lamb", [1, 512])
        self.w_in_a = self.din("w_in_a", [NA, D, 4 * D])
        self.w_out_a = self.din("w_out_a", [NA, D, D])
        self.w_kv = self.din("w_kv", [D, 2 * D])
        self.w_q_b = self.din("w_q_b", [2, D, D])
        self.w_out_b = self.din("w_out_b", [2, D, D])
        self.w_ffn_in = self.din("w_ffn_in", [DEPTH, D, 2 * DFF])
        self.w_ffn_out = self.din("w_ffn_out", [DEPTH, DFF, D])
        self.w_ple = self.din("w_ple", [DEPTH, PLE, D])
        self.w_ple_gate = self.din("w_ple_gate", [DEPTH, D, D])
        self.c_ident = self.din("c_ident", [128, 128])
        self.c_e1 = self.din("c_e1", [32, 512])
        self.c_tri = self.din("c_tri", [128, 128])
        self.c_smask = self.din("c_smask", [128, 512])
        self.y_p = self.dout("y_p", [NP, SEQ, D])
        self.y_s = self.dout("y_s", [1, DEC, D])
        self.k_p = self.dout("k_p", [NP, SEQ, D])
        self.v_p = self.dout("v_p", [NP, SEQ, D])
        self.k_s = self.dout("k_s", [1, DEC, D])
        self.v_s = self.dout("v_s", [1, DEC, D])
        self.hg_p = self.dout("hg_p", [NA, NP, H, 128, 128])
        self.hg_s = self.dout("hg_s", [NA, 1, H, 128, 128])
        self.cv_p = self.dout("cv_p", [DEPTH, NP, 2, DFF])
        self.cv_s = self.dout("cv_s", [DEPTH, 1, 2, DFF])
        self.scr = nc.dram_tensor("scr", [H, 128 * 513], F32, kind="Internal")
        self.t_scr = Tl("scr")

        sb = self.sb
        self.xT, self.t_xT = sb("xT", [128, NCH, TT_])
        self.t_xTc = [Tl("xT%d" % i) for i in range(NCH)]
        self.hT, self.t_hT = sb("hT", [128, NCH, TT_], BF16)
        self.t_hTc = [Tl("hT%d" % i) for i in range(NCH)]
        self.aT, self.t_aT = sb("aT", [128, 11, TT_], BF16)
        self.t_aTc = [Tl("aT%d" % i) for i in range(11)]
        self.mT, self.t_mT = sb("mT", [128, NCH, TT_])
        self.t_mTc = [Tl("mT%d" % i) for i in range(NCH)]
        self.KT, _ = sb("KT", [128, H, KMAX], BF16)
        self.t_KT = [Tl("KT%d" % i) for i in range(H)]
        NKT = KMAX // 128
        self.Vr, _ = sb("Vr", [128, NKT, D], BF16)
        self.t_Vr = [Tl("Vr%d" % i) for i in range(NKT)]
        self.wsl = []
        for i in range(NSLOT):
            self.wsl.append(sb("wsl%d" % i, [128, 1408], BF16))
        self.TB, self.t_TB = sb("TB", [128, 3, H, 128])
        self.ost = [sb("ost%d" % i, [128, D]) for i in range(2)]
        self.ost_i = 0
        names = ["s", "lf", "b", "enb", "qs", "ktf", "ocp", "rst", "eb"]
        self.tf = {n: sb("t_" + n, [128, max(TT_, 512) + 2]) for n in names}
        self.vTfb = sb("t_vTf", [128, max(TT_, 512) + 2])
        self.ebl = [sb("ebl%d" % i, [128, 8]) for i in range(2)]
        self.u8, _ = sb("u8", [128, 8, TT_], BF16)
        self.t_u8 = [Tl("u8_%d" % i) for i in range(8)]
        self.sq = [sb("sq%d" % i, [128, TT_], BF16) for i in range(2)]
        self.sq_i = 0
        self.vtok = [sb("vtok0", [128, max(TT_ // 128, 1), 128], BF16)] * 2
        self.kdtok = [sb("kdtok0", [128, max(TT_ // 128, 1), 128], BF16)] * 2
        self.Am = [sb("Am0", [128, max(TT_ // 128, 1), 128], BF16)] * 2
        self.plT, self.t_plT = sb("plT", [128, 2, TT_], BF16)
        self.S, _ = sb("S", [128, NA, H, 128])
        self.t_S = [[Tl("S%d_%d" % (l, h)) for h in range(H)] for l in range(NA)]
        self.Sb, _ = sb("Sb", [128, NA, H, 128], BF16)
        self.t_Sb = [[Tl("Sb%d_%d" % (l, h)) for h in range(H)] for l in range(NA)]
        self.ptr, self.t_ptr = sb("ptr", [128, 640])
        self.ident, self.t_ident = sb("ident", [128, 128])
        self.ones, self.t_ones = sb("ones", [128, 128], BF16)
        self.onesD, self.t_onesD = sb("onesD", [128, 128], BF16)
        self.onesH, self.t_onesH = sb("onesH", [128, 128], BF16)
        self.tri, self.t_tri = sb("tri", [128, 128])
        self.smask, self.t_smask = sb("smask", [128, 512])
        self.cs, _ = sb("cs", [128, DEPTH, 2, NFF])
        self.t_cs = [Tl("cs%d" % l) for l in range(DEPTH)]
        self.oml, self.t_oml = sb("oml", [128, NA, NCH])
        self.noml, self.t_noml = sb("noml", [128, NA, NCH])
        self.farb, self.t_farb = sb("farb", [128, H])
        self.lamt, self.t_lamt = self.tf["ktf"]
        self.nlam, self.t_nlam = sb("nlam", [128, 2])
        self.gs2, self.t_gs2 = sb("gs2", [128, 2, NCH])
        self.sm8, self.t_sm8 = sb("sm8", [128, 16])
        self.rbt, self.t_rbt = sb("rbt", [32, H])
        self.rbrep, self.t_rbrep = sb("rbrep", [32, 128])
        self.psw, self.t_psw = [], []
        for i in range(4):
            self.psw.append(self.es.enter_context(nc.psum_tensor("psw%d" % i, [128, 512], F32)))
            self.t_psw.append(Tl("psw%d" % i, psum=True))
        self.psq, self.t_psq = [], [Tl("psq%d" % i, psum=True) for i in range(4)]
        for i in range(4):
            self.psq.append(self.es.enter_context(nc.psum_tensor("psq%d" % i, [128, 512], F32)))
        self.pw_i = 0
        self.pq_i = 0
        self.pq_n = 4
        self.sc_i = 0

        self.o_gn = 0
        self.o_lb = 128
        self.o_ghg = 144
        self.o_gkv = 160
        self.o_gsl = 168
        self.o_cw = 184
        self.o_cb = 448

        seqs = [("p", i) for i in range(NP)] + [("s", 0)]
        self.wsched = []
        self.P.dry = True
        for kind, si in seqs:
            self.run_seq(kind, si)
        self.P.dry = False
        self.pw_i = self.pq_i = self.ost_i = self.sq_i = 0
        self.wpos = 0
        self.setup()
        for kind, si in seqs:
            self.run_seq(kind, si)
        assert self.wpos == len(self.wsched), (self.wpos, len(self.wsched))
        self.P.emit()
        self.es.close()
        return nc

    def pcol(self, off, n=1):
        return self.ptr[:, off:off + n]

    def setup(self):
        P = self.P
        self.DMA("sp", self.ident[:], self.c_ident.ap(), [], [self.t_ident], "c0")
        self.DMA("sp", self.tri[:], self.c_tri.ap(), [], [self.t_tri], "c1")
        self.DMA("sp", self.smask[:], self.c_smask.ap(), [], [self.t_smask], "c2")
        self.DMA("sp", self.farb[:], self.rb.ap()[15:16, :].partition_broadcast(128), [], [self.t_farb], "c3")
        self.DMA("sp", self.lamt[:, 0:512], self.lamb.ap().partition_broadcast(128), [], [self.t_lamt], "c4")
        self.DMA("sp", self.rbt[:], self.rb.ap(), [], [self.t_rbt], "c5")
        self.dve(lambda e: e.memset(self.ones[:], 1.0), [], [self.t_ones])
        self.dve(lambda e: e.memset(self.onesD[:], 1.0 / D), [], [self.t_onesD])
        self.dve(lambda e: e.memset(self.onesH[:], 1.0 / 128), [], [self.t_onesH])
        for i in range(5):
            st, t_st = self.ost[i % 2]
            self.DMA("sp", st[:, 0:128], self.ptab.ap()[i * 128:(i + 1) * 128, :], [], [t_st], "ost%d" % (i % 2))
            ps, t_ps = self.pq()
            self.TR(ps, st[:, 0:128], 128, [t_st], [t_ps])
            self.CP("dve", self.ptr[:, i * 128:(i + 1) * 128], ps, [t_ps], [self.t_ptr])
        self.dve(lambda e: e.memset(self.oml[:, 0, :], 1.0), [], [self.t_oml])
        self.TT(self.sm8[:, 0:8], self.pcol(self.o_lb, 8), self.pcol(self.o_lb + 8, 8), ALU.subtract,
                [self.t_ptr], [self.t_sm8])
        self.A(self.oml[:, 1, :], self.sm8[:, 0:8], AF.Sigmoid, [self.t_sm8], [self.t_oml])
        self.TS(self.noml[:].rearrange("p a b -> p (a b)"), self.oml[:].rearrange("p a b -> p (a b)"),
                -1.0, None, ALU.mult, None, [self.t_oml], [self.t_noml])
        for j in range(2):
            lam_init = 0.8 - 0.6 * math.exp(-0.3 * (NA + j))
            base = j * 256
            self.TT(self.lamt[:, base:base + 64], self.lamt[:, base:base + 64], self.lamt[:, base + 64:base + 128],
                    ALU.mult, [self.t_lamt], [self.t_lamt])
            self.TT(self.lamt[:, base + 128:base + 192], self.lamt[:, base + 128:base + 192],
                    self.lamt[:, base + 192:base + 256], ALU.mult, [self.t_lamt], [self.t_lamt])
            self.dve(lambda e, b=base: e.reduce_sum(out=self.sm8[:, 8:9], in_=self.lamt[:, b:b + 64], axis=AX.X),
                     [self.t_lamt], [self.t_sm8])
            self.dve(lambda e, b=base: e.reduce_sum(out=self.sm8[:, 9:10], in_=self.lamt[:, b + 128:b + 192], axis=AX.X),
                     [self.t_lamt], [self.t_sm8])
            self.A(self.sm8[:, 10:12], self.sm8[:, 8:10], AF.Exp, [self.t_sm8], [self.t_sm8])
            self.TT(self.sm8[:, 12:13], self.sm8[:, 11:12], self.sm8[:, 10:11], ALU.subtract, [self.t_sm8], [self.t_sm8])
            self.TS(self.nlam[:, j:j + 1], self.sm8[:, 12:13], -lam_init, None, ALU.add, None, [self.t_sm8], [self.t_nlam])
            self.TS(self.gs2[:, j, :], self.pcol(self.o_gsl + j * 8, 8), 1.0 - lam_init, None, ALU.mult, None,
                    [self.t_ptr], [self.t_gs2])
        e1, t_e1 = self.tf["s"]
        self.DMA("sp", e1[0:32, 0:512], self.c_e1.ap(), [], [t_e1], "c6")
        for h in range(H):
            self.CP("dve", self.rbrep[:], self.rbt[:, h:h + 1].to_broadcast([32, 128]), [self.t_rbt], [self.t_rbrep])
            ps, t_ps = self.pw()
            self.MM(ps[:, 0:512], [(self.rbrep[:], e1[0:32, 0:512])], [self.t_rbrep, t_e1], [t_ps])
            rep, t_rep = self.tf["lf" if h % 2 else "b"]
            self.CP("dve", rep[:, 0:512], ps[:, 0:512], [t_ps], [t_rep])
            dst = bass.AP(self.scr, h * 128 * 513, [[513, 128], [1, 512]])
            self.DMA("sp", dst, rep[:, 0:512], [t_rep], [self.t_scr], "scrw")
        for h in range(H):
            for vi in range(3):
                src = bass.AP(self.scr, h * 128 * 513 + 127 + 128 * vi, [[512, 128], [1, 128]])
                self.DMA("sp", self.TB[:, vi, h, :], src, [self.t_scr], [self.t_TB], "tb")
        self.dve(lambda e: e.memset(self.TB[64:128, 0, :, 0:64], NEG), [], [self.t_TB])

    def issue_w(self, i):
        desc = self.wsched[i]
        sl, t_sl = self.wsl[i % NSLOT]
        kind = desc[0]
        kc = 8
        if kind == "in":
            _, l, h, g = desc
            src = self.w_in_a.ap()[l, :, g * D + h * 128: g * D + (h + 1) * 128]
        elif kind == "sq":
            _, name, l, cb = desc
            src = getattr(self, name).ap()[l, :, cb * 128:(cb + 1) * 128]
        elif kind == "kv":
            _, cb = desc
            src = self.w_kv.ap()[:, cb * 128:(cb + 1) * 128]
        elif kind == "fi":
            _, l, j, g = desc
            src = self.w_ffn_in.ap()[l, :, g * DFF + j * 128: g * DFF + (j + 1) * 128]
        elif kind == "fo":
            _, l, kg, cb = desc
            src = self.w_ffn_out.ap()[l, kg * 1408:(kg + 1) * 1408, cb * 128:(cb + 1) * 128]
            kc = 11
        elif kind == "pl":
            _, l, cb = desc
            src = self.w_ple.ap()[l, :, cb * 128:(cb + 1) * 128]
            kc = 2
        src = src.rearrange("(c p) n -> p c n", p=128)
        out = sl[:, 0:kc * 128].rearrange("p (c n) -> p c n", c=kc)
        self.P.dma("pool", lambda e: [e.dma_start(out=out, in_=src)], [], [t_sl], "w%d" % (i % NSLOT))

    def wv(self, desc, kc=8):
        sl, t_sl = self.next_w(desc)
        return sl[:, 0:kc * 128].rearrange("p (c n) -> p c n", c=kc), t_sl

    def next_w(self, desc):
        if self.P.dry:
            self.wsched.append(desc)
            return self.wsl[0]
        i = self.wpos
        assert self.wsched[i] == desc, (self.wsched[i], desc)
        while self.wissued < min(len(self.wsched), i + NSLOT - 1):
            self.issue_w(self.wissued)
            self.wissued += 1
        self.wpos += 1
        return self.wsl[i % NSLOT]

    def rstd(self, chunks, n, ones, t_ones, bank=None):
        ps, t_ps = self.pw() if bank is None else (self.psq[bank], self.t_psq[bank])
        nchunk = len(chunks)
        for c, (ap, tl) in enumerate(chunks):
            sq, t_sq = self.sq[self.sq_i]
            self.sq_i ^= 1
            if c % 2 == 0:
                self.A(sq[:, 0:n], ap, AF.Square, tl, [t_sq])
            else:
                self.TT(sq[:, 0:n], ap, ap, ALU.mult, tl, [t_sq])
            self.MM(ps[:, 0:n], [(ones[:], sq[:, 0:n])], [t_sq, t_ones], [t_ps], start=(c == 0), stop=(c == nchunk - 1))
        rst, t_rst = self.tf["rst"]
        self.A(rst[:, 0:n], ps[:, 0:n], AF.Ln, [t_ps], [t_rst], bias=EPS)
        self.A(rst[:, 0:n], rst[:, 0:n], AF.Exp, [t_rst], [t_rst], scale=-0.5)
        return rst[:, 0:n], t_rst

    def stat_begin(self, bank):
        return {"ps": self.psq[bank], "t": self.t_psq[bank], "c": 0, "pend": None}

    def stat_flush(self, st):
        if st["pend"] is not None:
            st["pend"]()
            st["pend"] = None

    def stat_add(self, st, ap, tl, n, last):
        self.stat_flush(st)
        c = st["c"]
        sq, t_sq = self.sq[self.sq_i]
        self.sq_i ^= 1
        if c % 2 == 0:
            self.A(sq[:, 0:n], ap, AF.Square, tl, [t_sq])
        else:
            self.TT(sq[:, 0:n], ap, ap, ALU.mult, tl, [t_sq])
        st["pend"] = lambda: self.MM(st["ps"][:, 0:n], [(self.onesD[:], sq[:, 0:n])], [t_sq, self.t_onesD], [st["t"]],
                                     start=(c == 0), stop=last)
        st["c"] = c + 1

    def stat_finish(self, st, n, key):
        self.stat_flush(st)
        rst, t_rst = self.tf[key]
        self.A(rst[:, 0:n], st["ps"][:, 0:n], AF.Ln, [st["t"]], [t_rst], bias=EPS)
        self.A(rst[:, 0:n], rst[:, 0:n], AF.Exp, [t_rst], [t_rst], scale=-0.5)
        return rst[:, 0:n], t_rst

    def norm_to_h(self, n, goff, rst=None):
        if rst is None:
            st = self.stat_begin(2)
            for c in range(NCH):
                self.stat_add(st, self.xT[:, c, 0:n], [self.t_xTc[c]], n, c == NCH - 1)
            rst = self.stat_finish(st, n, "eb")
        rs, t_rs = rst
        for c in range(NCH):
            self.STT(self.hT[:, c, 0:n], self.xT[:, c, 0:n], self.pcol(goff + c), rs, ALU.mult, ALU.mult,
                     [self.t_xTc[c], t_rs, self.t_ptr], [self.t_hTc[c]])

    def resid_norm(self, n, goff, st_m, want_x_stats):
        rs, t_rs = self.stat_finish(st_m, n, "rst")
        st_x = self.stat_begin(2) if want_x_stats else None
        for c in range(NCH):
            tmp, t_tmp = self.tf["ocp" if c % 2 else "qs"]
            self.STT(tmp[:, 0:n], self.mT[:, c, 0:n], self.pcol(goff + c), rs, ALU.mult, ALU.mult,
                     [self.t_mTc[c], t_rs, self.t_ptr], [t_tmp])
            self.TT(self.xT[:, c, 0:n], self.xT[:, c, 0:n], tmp[:, 0:n], ALU.add, [self.t_xTc[c], t_tmp], [self.t_xTc[c]])
            if want_x_stats:
                self.stat_add(st_x, self.xT[:, c, 0:n], [self.t_xTc[c]], n, c == NCH - 1)
        if want_x_stats:
            return self.stat_finish(st_x, n, "eb")
        return None

    def proj_to_m(self, n, name, l, src, t_src):
        st = self.stat_begin(3)
        for c in range(8):
            wv, t_sl = self.wv(("sq", name, l, c))
            ps, t_ps = self.pw()
            self.MM(ps[:, 0:n], [(wv[:, k, :], src(k)) for k in range(NCH)], [t_sl] + t_src, [t_ps])
            self.CP("act", self.mT[:, c, 0:n], ps[:, 0:n], [t_ps], [self.t_mTc[c]])
            self.stat_add(st, self.mT[:, c, 0:n], [self.t_mTc[c]], n, c == 7)
        return st

    def run_seq(self, kind, si):
        cfg = self.cfg
        TT_, SEQ, PAST, DEC = cfg.tt, cfg.seq, cfg.past, cfg.dec
        L = SEQ if kind == "p" else DEC
        kbase = 0 if kind == "p" else PAST
        if kind == "p":
            for l in range(NA):
                for h in range(H):
                    self.dve(lambda e, l=l, h=h: e.memset(self.S[:, l, h, :], 0.0), [], [self.t_S[l][h]])
                    self.dve(lambda e, l=l, h=h: e.memset(self.Sb[:, l, h, :], 0.0), [], [self.t_Sb[l][h]])
            for l in range(DEPTH):
                self.dve(lambda e, l=l: e.memset(self.cs[:, l, :, :], 0.0), [], [self.t_cs[l]])
        else:
            for l in range(NA):
                self.DMA("sp", self.S[:, l, :, :], self.sh.ap()[l].rearrange("h k v -> k h v"), [], self.t_S[l], "sin%d" % l)
                for h in range(H):
                    self.CP("dve", self.Sb[:, l, h, :], self.S[:, l, h, :], [self.t_S[l][h]], [self.t_Sb[l][h]])
            for l in range(DEPTH):
                st, t_st = self.ost[self.ost_i]
                self.ost_i ^= 1
                self.DMA("sp", st[0:44, 0:128], self.sc.ap()[l], [], [t_st], "ost%d" % (self.ost_i ^ 1))
                ps, t_ps = self.pq()
                self.TR(ps[:, 0:44], st[0:44, 0:128], 44, [t_st], [t_ps])
                self.CP("dve", self.cs[:, l, :, :].rearrange("p a b -> p (a b)"), ps[:, 0:44], [t_ps], [self.t_cs[l]])
            for t in range(PAST // 128):
                st, t_st = self.ost[self.ost_i]
                oi = self.ost_i
                self.ost_i ^= 1
                self.DMA("sp", st[:], self.ck.ap()[t * 128:(t + 1) * 128, :], [], [t_st], "ost%d" % oi)
                for hh in range(2):
                    ps, t_ps = self.pw()
                    for q in range(4):
                        h = hh * 4 + q
                        self.TR(ps[:, q * 128:(q + 1) * 128], st[:, h * 128:(h + 1) * 128], 128, [t_st], [t_ps])
                    self.CP("dve", self.KT[:, hh * 4:(hh + 1) * 4, t * 128:(t + 1) * 128],
                            ps[:].rearrange("p (a b) -> p a b", a=4), [t_ps], self.t_KT[hh * 4:(hh + 1) * 4])
                st, t_st = self.ost[self.ost_i]
                oi = self.ost_i
                self.ost_i ^= 1
                self.DMA("sp", st[:], self.cv.ap()[t * 128:(t + 1) * 128, :], [], [t_st], "ost%d" % oi)
                self.CP("act", self.Vr[:, t, :], st[:], [t_st], [self.t_Vr[t]])
        for t0 in range(0, L, TT_):
            n = min(TT_, L - t0)
            self.run_tile(kind, si, t0, n, kbase, last=(t0 + n >= L))

    def load_T(self, src_rows, n, dst_fn, nchunk, t_dst, eng="dve"):
        for sub in range((n + 127) // 128):
            m = min(128, n - sub * 128)
            st, t_st = self.ost[self.ost_i]
            oi = self.ost_i
            self.ost_i ^= 1
            self.DMA("sp", st[0:m, 0:nchunk * 128], src_rows(sub, m), [], [t_st], "ost%d" % oi)
            for g0 in range(0, nchunk, 4):
                g = min(4, nchunk - g0)
                ps, t_ps = self.pw()
                for q in range(g):
                    self.TR(ps[:, q * 128:q * 128 + m], st[0:m, (g0 + q) * 128:(g0 + q + 1) * 128], m, [t_st], [t_ps])
                self.CP(eng, dst_fn(g0, g0 + g, sub * 128, m),
                        ps[:, 0:g * 128].rearrange("p (a b) -> p a b", a=g)[:, :, 0:m], [t_ps], t_dst)

    def store_T(self, n, src_fn, t_src, dst_rows, sem_prefix, also=None):
        for sub in range((n + 127) // 128):
            m = min(128, n - sub * 128)
            st, t_st = self.ost[self.ost_i]
            oi = self.ost_i
            self.ost_i ^= 1
            for g0 in (0, 4):
                ps, t_ps = self.pw()
                for q in range(4):
                    c = g0 + q
                    self.TR(ps[0:m, q * 128:(q + 1) * 128], src_fn(c, sub * 128, m), 128, t_src(c), [t_ps])
                self.CP("act" if g0 else "dve", st[0:m, g0 * 128:(g0 + 4) * 128], ps[0:m, :], [t_ps], [t_st])
                if also is not None:
                    also(sub, m, g0, ps, t_ps)
            self.DMA("sp", dst_rows(sub, m), st[0:m, :], [t_st], [], "ost%d" % oi)

    def run_tile(self, kind, si, t0, n, kbase, last):
        cfg = self.cfg
        xsrc = self.xp if kind == "p" else self.xs
        self.load_T(lambda sub, m: xsrc.ap()[si, t0 + sub * 128: t0 + sub * 128 + m, :], n,
                    lambda c0, c1, col, m: self.xT[:, c0:c1, col:col + m], NCH, self.t_xTc)
        xr = None
        for l in range(cfg.nlayers):
            g0 = self.o_gn + l * 32
            self.norm_to_h(n, g0, xr)
            if l < NA:
                self.hgrn(l, n)
                st = self.proj_to_m(n, "w_out_a", l, lambda k: self.aT[:, k, 0:n], self.t_aTc[0:8])
            else:
                self.attn(l, kind, t0, n, kbase)
                st = self.proj_to_m(n, "w_out_b", l - NA, lambda k: self.aT[:, k, 0:n], self.t_aTc[0:8])
            xr = self.resid_norm(n, g0 + 8, st, True)
            self.ple_load(l, kind, si, t0, n)
            self.norm_to_h(n, g0 + 16, xr)
            st = self.ffn(l, n)
            self.resid_norm(n, g0 + 24, st, False)
            xr = self.ple(l, kind, si, t0, n)
            if l == NA - 1:
                self.kv(kind, si, t0, n, kbase, xr)
        ydst = self.y_p if kind == "p" else self.y_s
        self.store_T(n, lambda c, col, m: self.xT[:, c, col:col + m], lambda c: [self.t_xTc[c]],
                     lambda sub, m: ydst.ap()[si, t0 + sub * 128:t0 + sub * 128 + m, :], "y")
        if last:
            hdst = self.hg_p if kind == "p" else self.hg_s
            for l in range(NA):
                self.DMA("sp", hdst.ap()[l, si].rearrange("h k v -> k h v"), self.S[:, l, :, :], self.t_S[l], [], "sout%d" % l)
            cdst = self.cv_p if kind == "p" else self.cv_s
            for l in range(DEPTH):
                ps, t_ps = self.pq()
                self.TR(ps[0:44, :], self.cs[:, l, :, :].rearrange("p a b -> p (a b)"), 128, [self.t_cs[l]], [t_ps])
                st, t_st = self.ost[self.ost_i]
                oi = self.ost_i
                self.ost_i ^= 1
                self.CP("dve", st[0:44, 0:128], ps[0:44, :], [t_ps], [t_st])
                self.P.dma("sp", lambda e, l=l, st=st: [
                    e.dma_start(out=cdst.ap()[l, si, r, :].rearrange("(j p) -> j p", p=128), in_=st[r * 22:(r + 1) * 22, 0:128])
                    for r in range(2)], [t_st], [], "ost%d" % oi, nd=2)

    def hgrn(self, l, n):
        tf = self.tf
        nsub = (n + 127) // 128
        C = 64 if n >= 64 else n
        nchk = n // C
        s, t_s = tf["s"]
        lf, t_lf = tf["lf"]
        b, t_b = tf["b"]
        eb, t_eb = tf["eb"]
        enb, t_enb = tf["enb"]
        qs, t_qs = tf["qs"]
        ktf, t_ktf = tf["ktf"]
        kdf, t_kdf = tf["b"]
        vTf, t_vTf = self.vTfb
        vtok, t_vtok = self.vtok[0]
        kdtok, t_kdtok = self.kdtok[0]
        Am, t_Am = self.Am[0]

        def bufs(h):
            par = h % 2
            return ((self.u8[:, 0 + par, :], self.t_u8[0 + par]), (self.u8[:, 2 + par, :], self.t_u8[2 + par]),
                    (self.u8[:, 4 + par, :], self.t_u8[4 + par]), self.ebl[par])

        def Pg(h, key):
            g_ = {"q": 0, "f": 1, "i": 2, "g": 3}[key]
            wv, t_sl = self.wv(("in", l, h, g_))
            ps, t_ps = self.pw()
            self.MMh(ps, wv, t_sl, t_ps, n, h == 0 and key == "f")
            return ps, t_ps

        def E1a(h, bf, bq):
            (psf, t_psf), (psq_, t_psq_) = bf, bq
            self.A(s[:, 0:n], psf[:, 0:n], AF.Sigmoid, [t_psf], [t_s], scale=-1.0)
            self.A(qs[:, 0:n], psq_[:, 0:n], AF.Silu, [t_psq_], [t_qs])

        def E1b(h, bi, bg):
            _, _, (gate, t_gate), _ = bufs(h)
            (psi, t_psi), (psg, t_psg) = bi, bg
            self.A(gate[:, 0:n], psg[:, 0:n], AF.Silu, [t_psg], [t_gate])
            self.CP("dve", vTf[:, 0:n], psi[:, 0:n], [t_psi], [t_vTf])

        def E2steps(h):
            (qt, t_qt), (kt, t_kt), (gate, t_gate), (ebl, t_ebl) = bufs(h)
            ebv = eb[:, 0:n].rearrange("p (c t) -> p c t", t=C)[:, :, C - 1:C]
            return [
                lambda: self.A(lf[:, 0:n], s[:, 0:n], AF.Ln, [t_s, self.t_noml], [t_lf],
                               scale=self.noml[:, l, h:h + 1], bias=1.0),
                lambda: self.dve(lambda e: e.tensor_tensor_scan(out=b[:, 0:n], data0=self.smask[:, 0:n], data1=lf[:, 0:n],
                                                                initial=0.0, op0=ALU.mult, op1=ALU.add),
                                 [t_lf, self.t_smask], [t_b]),
                lambda: (self.A(eb[:, 0:n], b[:, 0:n], AF.Exp, [t_b], [t_eb]),
                         self.A(enb[:, 0:n], b[:, 0:n], AF.Exp, [t_b], [t_enb], scale=-1.0)),
                lambda: self.TT(qt[:, 0:n], qs[:, 0:n], eb[:, 0:n], ALU.mult, [t_qs, t_eb], [t_qt]),
                lambda: self.STT(ktf[:, 0:n], s[:, 0:n], self.oml[:, l, h:h + 1], enb[:, 0:n], ALU.mult, ALU.mult,
                                 [t_s, t_enb, self.t_oml], [t_ktf]),
                lambda: (self.CP("dve", kt[:, 0:n], ktf[:, 0:n], [t_ktf], [t_kt]),
                         self.CP("dve", ebl[:, 0:nchk].unsqueeze(2), ebv, [t_eb], [t_ebl])),
                lambda: self.TT(kdf[:, 0:n].rearrange("p (c t) -> p c t", t=C),
                                ktf[:, 0:n].rearrange("p (c t) -> p c t", t=C),
                                ebv.to_broadcast([128, nchk, C]), ALU.mult, [t_ktf, t_eb], [t_kdf]),
            ]

        def E2(h):
            for f_ in E2steps(h):
                f_()

        def T(h):
            (qt, t_qt), (kt, t_kt), _, _ = bufs(h)
            for sub in range(nsub):
                m = min(128, n - sub * 128)
                ps, t_ps = self.pq()
                self.TR(ps[0:m, :], vTf[:, sub * 128:sub * 128 + m], 128, [t_vTf], [t_ps])
                self.CP("dve", vtok[0:m, sub, :], ps[0:m, :], [t_ps], [t_vtok])
                ps, t_ps = self.pq()
                self.TR(ps[0:m, :], kdf[:, sub * 128:sub * 128 + m], 128, [t_kdf], [t_ps])
                self.CP("act", kdtok[0:m, sub, :], ps[0:m, :], [t_ps], [t_kdtok])
                ps, t_ps = self.pq()
                self.MM(ps[0:m, 0:m], [(kt[:, sub * 128:sub * 128 + m], qt[:, sub * 128:sub * 128 + m])], [t_kt, t_qt], [t_ps])
                self.TT(Am[0:m, sub, 0:m], ps[0:m, 0:m], self.tri[0:m, 0:m], ALU.mult, [t_ps, self.t_tri], [t_Am])

        def G(h, filler=None):
            (qt, t_qt), _, _, (ebl, t_ebl) = bufs(h)
            pso, t_pso = self.psq[2], self.t_psq[2]
            t_S, t_Sb = self.t_S[l][h], self.t_Sb[l][h]
            for c in range(nchk):
                sub, off = (c * C) // 128, (c * C) % 128
                m = min(128, n - sub * 128)
                cols = slice(c * C, (c + 1) * C)
                self.MM(pso[:, cols], [(self.Sb[:, l, h, :], qt[:, cols]),
                                       (vtok[0:m, sub, :], Am[0:m, sub, off:off + C])],
                        [t_Sb, t_qt, t_vtok, t_Am], [t_pso])
                ps, t_ps = self.pq()
                self.MM(ps, [(kdtok[off:off + C, sub, :], vtok[off:off + C, sub, :])], [t_kdtok, t_vtok], [t_ps])
                self.STT(self.Sb[:, l, h, :], self.S[:, l, h, :], ebl[:, c:c + 1], ps, ALU.mult, ALU.add,
                         [t_S, t_ebl, t_ps], [t_Sb])
                self.STT(self.S[:, l, h, :], self.S[:, l, h, :], ebl[:, c:c + 1], ps, ALU.mult, ALU.add,
                         [t_S, t_ebl, t_ps], [t_S])
                if filler is not None:
                    filler(c)
            return pso, t_pso

        def N(h, pso, t_pso):
            _, _, (gate, t_gate), _ = bufs(h)
            ocp, t_ocp = tf["ocp"]
            self.CP("act", ocp[:, 0:n], pso[:, 0:n], [t_pso], [t_ocp])
            rst, t_rst = self.rstd([(ocp[:, 0:n], [t_ocp])], n, self.onesH, self.t_onesH, bank=2)
            self.STT(ocp[:, 0:n], ocp[:, 0:n], self.pcol(self.o_ghg + l * 8 + h), rst, ALU.mult, ALU.mult,
                     [t_ocp, t_rst, self.t_ptr], [t_ocp])
            self.TT(self.aT[:, h, 0:n], ocp[:, 0:n], gate[:, 0:n], ALU.mult, [t_ocp, t_gate], [self.t_aTc[h]])

        self.pq_n = 3
        bf, bq = Pg(0, "f"), Pg(0, "q")
        bi, bg = Pg(0, "i"), Pg(0, "g")
        E1a(0, bf, bq)
        E1b(0, bi, bg)
        E2(0)
        per = -(-16 // nchk)
        for h in range(H):
            nxt = h + 1 < H
            if nxt:
                bf, bq = Pg(h + 1, "f"), Pg(h + 1, "q")
            T(h)
            filler = None
            if nxt:
                E1a(h + 1, bf, bq)
                st_ = {"pos": 0, "banks": {}, "wv": {}}

                def filler(c, st_=st_, h=h):
                    todo = per if c < nchk - 1 else 16 - st_["pos"]
                    while todo > 0 and st_["pos"] < 16:
                        key = "i" if st_["pos"] < 8 else "g"
                        k0 = st_["pos"] % 8
                        if k0 == 0:
                            st_["wv"][key] = self.wv(("in", l, h + 1, 2 if key == "i" else 3))
                            st_["banks"][key] = self.pw()
                        k1 = min(8, k0 + todo)
                        wv, t_sl = st_["wv"][key]
                        ps, t_ps = st_["banks"][key]
                        self.MM(ps[:, 0:n], [(wv[:, k, :], self.hT[:, k, 0:n]) for k in range(k0, k1)],
                                [t_sl] + self.t_hTc[k0:k1], [t_ps], start=(k0 == 0), stop=(k1 == 8))
                        todo -= k1 - k0
                        st_["pos"] += k1 - k0
            hook = filler
            if nxt:
                steps = E2steps(h + 1)
                sp = {"i": 0}

                def hook(c, filler=filler, steps=steps, sp=sp):
                    filler(c)
                    upto = len(steps) if c == nchk - 1 else -(-(c + 1) * len(steps) // nchk)
                    while sp["i"] < upto:
                        steps[sp["i"]]()
                        sp["i"] += 1
            pso, t_pso = G(h, hook)
            N(h, pso, t_pso)
            if nxt:
                E1b(h + 1, st_["banks"]["i"], st_["banks"]["g"])
        self.pq_n = 4

    def ffn(self, l, n):
        tf = self.tf
        cw = self.o_cw + l * 3 * NFF
        cb = self.o_cb + l * NFF
        st = self.stat_begin(3)
        Gs = [tf["s"], tf["qs"]]
        accs = [tf["lf"], tf["enb"]]
        sgs = [tf["b"], tf["ktf"]]

        def stage_a(j):
            G, t_G = Gs[j % 2]
            acc, t_acc = accs[j % 2]
            wv, t_sl = self.wv(("fi", l, j, 0))
            psg, t_psg = self.pw()
            self.MMh(psg, wv, t_sl, t_psg, n, j == 0)
            wv, t_sl = self.wv(("fi", l, j, 1))
            psu, t_psu = self.pw()
            self.MMh(psu, wv, t_sl, t_psu, n, False)
            self.CP("dve", G[:, 0:2], self.cs[:, l, :, j], [self.t_cs[l]], [t_G])
            self.CP("act", G[:, 2:2 + n], psg[:, 0:n], [t_psg], [t_G])
            self.A(acc[:, 0:n], psg[:, 0:n], AF.Identity, [t_psg, self.t_ptr], [t_acc],
                   scale=self.pcol(cw + 2 * NFF + j), bias=self.pcol(cb + j))
            self.CP("dve", self.cs[:, l, :, j], G[:, n:n + 2], [t_G], [self.t_cs[l]])
            self.STT(acc[:, 0:n], G[:, 1:1 + n], self.pcol(cw + NFF + j), acc[:, 0:n], ALU.mult, ALU.add,
                     [t_G, t_acc, self.t_ptr], [t_acc])
            self.STT(acc[:, 0:n], G[:, 0:n], self.pcol(cw + j), acc[:, 0:n], ALU.mult, ALU.add,
                     [t_G, t_acc, self.t_ptr], [t_acc])
            return psu, t_psu

        def stage_b(j, jj, psu, t_psu):
            acc, t_acc = accs[j % 2]
            sg, t_sg = sgs[j % 2]
            self.A(sg[:, 0:n], acc[:, 0:n], AF.Silu, [t_acc], [t_sg])
            self.TT(self.aT[:, jj, 0:n], sg[:, 0:n], psu[:, 0:n], ALU.mult, [t_sg, t_psu], [self.t_aTc[jj]])

        for kg in range(2):
            prev = None
            for jj in range(11):
                j = kg * 11 + jj
                cur = (j, jj) + stage_a(j)
                if prev is not None:
                    stage_b(*prev)
                prev = cur
            stage_b(*prev)
            for c in range(8):
                wv, t_sl = self.wv(("fo", l, kg, c), kc=11)
                ps, t_ps = self.pw()
                self.MM(ps[:, 0:n], [(wv[:, k, :], self.aT[:, k, 0:n]) for k in range(11)], [t_sl] + self.t_aTc, [t_ps])
                if kg == 0:
                    self.CP("act", self.mT[:, c, 0:n], ps[:, 0:n], [t_ps], [self.t_mTc[c]])
                else:
                    self.TT(self.mT[:, c, 0:n], self.mT[:, c, 0:n], ps[:, 0:n], ALU.add,
                            [self.t_mTc[c], t_ps], [self.t_mTc[c]])
                    self.stat_add(st, self.mT[:, c, 0:n], [self.t_mTc[c]], n, c == 7)
        return st

    def ple_load(self, l, kind, si, t0, n):
        psrc = self.pp if kind == "p" else self.psm
        self.load_T(lambda sub, m: psrc.ap()[l, si, t0 + sub * 128:t0 + sub * 128 + m, :], n,
                    lambda c0, c1, col, m: self.plT[:, c0:c1, col:col + m], 2, [self.t_plT], eng="act")

    def ple(self, l, kind, si, t0, n):
        for c in range(NCH):
            self.CP("act" if c % 2 else "dve", self.hT[:, c, 0:n], self.xT[:, c, 0:n], [self.t_xTc[c]], [self.t_hTc[c]])
        st = self.stat_begin(2)
        sg, t_sg = self.tf["enb"]
        for c in range(8):
            wp, t_slp = self.wv(("pl", l, c), kc=2)
            wv, t_sl = self.wv(("sq", "w_ple_gate", l, c))
            psg, t_psg = self.pw()
            self.MMh(psg, wv, t_sl, t_psg, n, c == 0)
            psp, t_psp = self.pw()
            self.MM(psp[:, 0:n], [(wp[:, k, :], self.plT[:, k, 0:n]) for k in range(2)], [t_slp, self.t_plT], [t_psp])
            self.A(sg[:, 0:n], psg[:, 0:n], AF.Sigmoid, [t_psg], [t_sg])
            self.TT(sg[:, 0:n], sg[:, 0:n], psp[:, 0:n], ALU.mult, [t_sg, t_psp], [t_sg])
            self.TT(self.xT[:, c, 0:n], self.xT[:, c, 0:n], sg[:, 0:n], ALU.add, [self.t_xTc[c], t_sg], [self.t_xTc[c]])
            self.stat_add(st, self.xT[:, c, 0:n], [self.t_xTc[c]], n, c == 7)
        return self.stat_finish(st, n, "eb")

    def kv(self, kind, si, t0, n, kbase, rst):
        self.norm_to_h(n, self.o_gkv, rst)
        kpos = kbase + t0
        for cb in range(16):
            wv, t_sl = self.wv(("kv", cb))
            c = cb % 8
            ps, t_ps = self.pw()
            self.MMh(ps, wv, t_sl, t_ps, n, cb == 0)
            self.CP("act", self.mT[:, c, 0:n], ps[:, 0:n], [t_ps], [self.t_mTc[c]])
            if cb < 8:
                self.CP("act", self.KT[:, c, kpos:kpos + n], ps[:, 0:n], [t_ps], [self.t_KT[c]])
            if cb == 7:
                kdst = self.k_p if kind == "p" else self.k_s
                self.store_T(n, lambda c, col, m: self.mT[:, c, col:col + m], lambda c: [self.t_mTc[c]],
                             lambda sub, m: kdst.ap()[si, t0 + sub * 128:t0 + sub * 128 + m, :], "k")
            if cb == 15:
                vdst = self.v_p if kind == "p" else self.v_s

                def also(sub, m, g0, ps, t_ps):
                    kt_ = (kpos + sub * 128) // 128
                    po = (kpos + sub * 128) % 128
                    self.CP("dve" if g0 else "act", self.Vr[po:po + m, kt_, g0 * 128:(g0 + 4) * 128], ps[0:m, :],
                            [t_ps], [self.t_Vr[kt_]])
                self.store_T(n, lambda c, col, m: self.mT[:, c, col:col + m], lambda c: [self.t_mTc[c]],
                             lambda sub, m: vdst.ap()[si, t0 + sub * 128:t0 + sub * 128 + m, :], "v", also=also)

    def attn(self, l, kind, t0, n, kbase):
        j = l - NA
        tf = self.tf
        for c in range(8):
            wv, t_sl = self.wv(("sq", "w_q_b", j, c))
            ps, t_ps = self.pw()
            self.MMh(ps, wv, t_sl, t_ps, n, c == 0)
            self.CP("act" if c % 2 else "dve", self.u8[:, c, 0:n], ps[:, 0:n], [t_ps], [self.t_u8[c]])
        nsub = (n + 127) // 128
        qpos0 = kbase + t0
        kend = qpos0 + n
        ktiles = [(kt_, min(128, kend - kt_ * 128)) for kt_ in range((kend + 127) // 128)]
        nk = len(ktiles)
        flat = lambda t: t[:].rearrange("p a b -> p (a b)")
        pbs = [(flat(self.vtok[0][0]), self.vtok[0][1]), (flat(self.kdtok[0][0]), self.kdtok[0][1]),
               (flat(self.Am[0][0]), self.Am[0][1]), (self.plT[:, 0, :], self.t_plT)]
        (pv0, t_pv0), (pv1, t_pv1), (z0, t_z0), (z1, t_z1) = [(self.psq[q], self.t_psq[q]) for q in range(4)]
        scts = [tf["s"], tf["lf"]]
        for h in range(H):
            def scores(ki):
                kt_, kn = ktiles[ki]
                delta = kt_ * 128 - qpos0
                qa = max(0, delta)
                s0 = qa // 128
                s_far = min(nsub, max(s0, (delta + 384) // 128))
                c_far = min(n, s_far * 128)
                for cmap in range(2):
                    bi = (ki % 2) * 2 + cmap
                    sc, t_sc = self.psw[bi], self.t_psw[bi]
                    rows = slice(cmap * 64, (cmap + 1) * 64)
                    self.MM(sc[0:kn, qa:n], [(self.KT[rows, h, kt_ * 128:kt_ * 128 + kn], self.u8[rows, h, qa:n])],
                            [self.t_KT[h], self.t_u8[h]], [t_sc])
                for cmap in range(2):
                    bi = (ki % 2) * 2 + cmap
                    sc, t_sc = self.psw[bi], self.t_psw[bi]
                    Pb, t_Pb = pbs[bi]
                    if c_far > qa:
                        sct, t_sct = scts[cmap]
                        for s_ in range(s0, s_far):
                            m = min(128, n - s_ * 128)
                            vi = {0: 0, -128: 1, -256: 2}[delta - 128 * s_]
                            cs_ = slice(s_ * 128, s_ * 128 + m)
                            self.STT(sct[0:kn, cs_], sc[0:kn, cs_], 0.125, self.TB[0:kn, vi, h, 0:m], ALU.mult, ALU.add,
                                     [t_sc, self.t_TB], [t_sct])
                        self.A(Pb[0:kn, qa:c_far], sct[0:kn, qa:c_far], AF.Exp, [t_sct], [t_Pb])
                    if n > c_far:
                        self.A(Pb[0:kn, c_far:n], sc[0:kn, c_far:n], AF.Exp, [t_sc, self.t_farb], [t_Pb],
                               scale=0.125, bias=self.farb[0:kn, h:h + 1])

            def pvz(ki):
                kt_, kn = ktiles[ki]
                qa = max(0, kt_ * 128 - qpos0)
                st = (ki == 0)
                for cmap, (pv, t_pv, z, t_z) in enumerate(((pv0, t_pv0, z0, t_z0), (pv1, t_pv1, z1, t_z1))):
                    Pb, t_Pb = pbs[(ki % 2) * 2 + cmap]
                    self.MM(pv[:, qa:n], [(self.Vr[0:kn, kt_, h * 128:(h + 1) * 128], Pb[0:kn, qa:n])],
                            [self.t_Vr[kt_], t_Pb], [t_pv], start=st, stop=False, skip=True)
                    self.MM(z[:, qa:n], [(self.ones[0:kn, :], Pb[0:kn, qa:n])], [self.t_ones, t_Pb], [t_z],
                            start=st, stop=False, skip=True)

            scores(0)
            for ki in range(nk):
                if ki + 1 < nk:
                    scores(ki + 1)
                pvz(ki)
            rz0, t_rz0 = tf["b"]
            rz1, t_rz1 = tf["enb"]
            t1, t_t1 = tf["qs"]
            o0, t_o0 = tf["ktf"]
            ocp, t_ocp = tf["ocp"]
            self.A(rz0[:, 0:n], z0[:, 0:n], AF.Ln, [t_z0], [t_rz0])
            self.A(rz1[:, 0:n], z1[:, 0:n], AF.Ln, [t_z1], [t_rz1])
            self.A(rz0[:, 0:n], rz0[:, 0:n], AF.Exp, [t_rz0], [t_rz0], scale=-1.0)
            self.A(rz1[:, 0:n], rz1[:, 0:n], AF.Exp, [t_rz1], [t_rz1], scale=-1.0)
            self.STT(t1[:, 0:n], pv1[:, 0:n], self.nlam[:, j:j + 1], rz1[:, 0:n], ALU.mult, ALU.mult,
                     [t_pv1, t_rz1, self.t_nlam], [t_t1])
            self.TT(o0[:, 0:n], pv0[:, 0:n], rz0[:, 0:n], ALU.mult, [t_pv0, t_rz0], [t_o0])
            self.TT(ocp[:, 0:n], o0[:, 0:n], t1[:, 0:n], ALU.add, [t_o0, t_t1], [t_ocp])
            rst, t_rst = self.rstd([(ocp[:, 0:n], [t_ocp])], n, self.onesH, self.t_onesH)
            self.STT(self.aT[:, h, 0:n], ocp[:, 0:n], self.gs2[:, j, h:h + 1], rst, ALU.mult, ALU.mult,
                     [t_ocp, t_rst, self.t_gs2], [self.t_aTc[h]])


def pack_ptab(g_norms, lb_raw, g_hg, g_kv, g_subln, conv_w, conv_b):
    rows = [g_norms.reshape(-1, 128), lb_raw.reshape(-1, 128), g_hg.reshape(-1, 128), g_kv.reshape(-1, 128),
            g_subln.reshape(-1, 128), conv_w.reshape(-1, 128), conv_b.reshape(-1, 128)]
    t = np.concatenate(rows, axis=0).astype(np.float32)
    out = np.zeros((640, 128), np.float32)
    out[:t.shape[0]] = t
    return out


def make_in_maps(cfg, ncores, x_prompt, x_sample, cache_k, cache_v, state_hgrn, state_conv, p_prompt, p_sample,
                 g_norms, w_in_a, lb_raw, g_hg, w_out_a, g_kv, w_kv, rel_bias, w_q_b, lam_b, g_subln, w_out_b,
                 w_ffn_in, conv_w, conv_b, w_ffn_out, w_ple, w_ple_gate):
    f = lambda a: np.ascontiguousarray(np.asarray(a, dtype=np.float32))
    cst = _consts(cfg)
    shared = dict(
        ptab=pack_ptab(f(g_norms), f(lb_raw), f(g_hg), f(g_kv), f(g_subln), f(conv_w), f(conv_b)),
        rb=f(rel_bias), lamb=f(lam_b).reshape(1, 512),
        w_in_a=f(w_in_a), w_out_a=f(w_out_a), w_kv=f(w_kv), w_q_b=f(w_q_b), w_out_b=f(w_out_b),
        w_ffn_in=f(w_ffn_in), w_ffn_out=f(w_ffn_out), w_ple=f(w_ple), w_ple_gate=f(w_ple_gate),
        c_ident=cst["ident"], c_e1=cst["e1"], c_tri=cst["tri"], c_smask=cst["smask"])
    NP = cfg.nprompt
    maps = []
    for c in range(ncores):
        m = dict(shared)
        m["xp"] = f(x_prompt[c * NP:(c + 1) * NP])
        m["xs"] = f(x_sample[c:c + 1])
        m["ck"] = f(cache_k[c]).reshape(cfg.past, D)
        m["cv"] = f(cache_v[c]).reshape(cfg.past, D)
        m["sh"] = f(state_hgrn[:, c])
        m["sc"] = f(state_conv[:, c]).reshape(DEPTH, 2 * NFF, 128)
        m["pp"] = f(p_prompt[:, c * NP:(c + 1) * NP])
        m["psm"] = f(p_sample[:, c:c + 1])
        maps.append(m)
    return maps


def gather(cfg, res, ncores):
    cat = lambda k, ax: np.concatenate([np.asarray(r[k]) for r in res], axis=ax)
    NP, SEQ, DEC = cfg.nprompt, cfg.seq, cfg.dec
    y_p = cat("y_p", 0)
    y_s = cat("y_s", 0)
    k_p = cat("k_p", 0).reshape(ncores * NP, SEQ, H, 128)
    v_p = cat("v_p", 0).reshape(ncores * NP, SEQ, H, 128)
    k_s = cat("k_s", 0).reshape(ncores, DEC, H, 128)
    v_s = cat("v_s", 0).reshape(ncores, DEC, H, 128)
    hg_p = cat("hg_p", 1)
    hg_s = cat("hg_s", 1)
    cv_p = cat("cv_p", 1)
    cv_s = cat("cv_s", 1)
    return tuple(np.ascontiguousarray(a, dtype=np.float32)
                 for a in (y_p, y_s, k_p, v_p, k_s, v_s, hg_p, hg_s, cv_p, cv_s))


def kernel(**inputs):
    cfg = Cfg()
    ncores = 8
    nc = KB(cfg).build()
    maps = make_in_maps(cfg, ncores, **inputs)
    res = run_bass_kernel_spmd(nc, maps, core_ids=list(range(ncores)))
    return gather(cfg, res.results, ncores)
```
